# Optimizing a Trainium2 kernel written in Bass

```python
import math
import jax, jax.numpy as jnp
from jax import lax
import numpy as np

D_MODEL = 1024
BATCH = 8
SEQ = 4096
DEPTH = 1

MIX_WIDTH = D_MODEL
HEAD_DIM = 64
SWA_Q_HEADS = 8
SWA_KV_HEADS = 2
SWA_GROUP = SWA_Q_HEADS // SWA_KV_HEADS
WINDOW = 128
BLOCK = 128
DIFF_HEADS = 4
DIFF_VDIM = 2 * HEAD_DIM
Q_BLOCK = 128
SWA_Q_COLS = SWA_Q_HEADS * HEAD_DIM
SWA_KV_COLS = SWA_KV_HEADS * HEAD_DIM
DIFF_QK_COLS = DIFF_HEADS * 2 * HEAD_DIM
DIFF_V_COLS = DIFF_HEADS * DIFF_VDIM
IN_COLS = SWA_Q_COLS + 2 * SWA_KV_COLS + 2 * DIFF_QK_COLS + DIFF_V_COLS
SWA_OUT = SWA_Q_HEADS * HEAD_DIM
DIFF_OUT = DIFF_HEADS * DIFF_VDIM
MEM_LEN = 256
CROSS_HEADS = 4
CROSS_HEAD_DIM = D_MODEL // CROSS_HEADS
D_FF = 4 * D_MODEL
ROPE_THETA = 10000.0
NORM_EPS = 1e-5

kernel_name = "hymba_swa_sink_diffattn_xattn_sqrelu"


def rms_norm(x, g, eps=NORM_EPS):
    xf = x.astype(jnp.float32)
    y = xf * lax.rsqrt(jnp.mean(xf * xf, axis=-1, keepdims=True) + eps)
    return (y * g.astype(jnp.float32)).astype(x.dtype)


def rope_tables(positions, dim):
    inv_freq = ROPE_THETA ** (-jnp.arange(0, dim, 2, dtype=jnp.float32) / dim)
    ang = positions.astype(jnp.float32)[..., None] * inv_freq
    return jnp.cos(ang), jnp.sin(ang)


def apply_rope(x, cos, sin):
    shp = cos.shape[:2] + (1,) * (x.ndim - 3) + cos.shape[-1:]
    c = cos.reshape(shp).astype(x.dtype)
    s = sin.reshape(shp).astype(x.dtype)
    x1, x2 = jnp.split(x, 2, axis=-1)
    return jnp.concatenate([x1 * c - x2 * s, x2 * c + x1 * s], axis=-1)


def sliding_window_sink_attention(q, k, v, sinks):
    Bn, S = q.shape[0], q.shape[1]
    nb = S // BLOCK
    scale = HEAD_DIM ** -0.5
    qb = q.reshape(Bn, nb, BLOCK, SWA_KV_HEADS, SWA_GROUP, HEAD_DIM)
    kb = k.reshape(Bn, nb, BLOCK, SWA_KV_HEADS, HEAD_DIM)
    vb = v.reshape(Bn, nb, BLOCK, SWA_KV_HEADS, HEAD_DIM)
    pad = ((0, 0), (1, 0), (0, 0), (0, 0), (0, 0))
    kw = jnp.concatenate([jnp.pad(kb, pad)[:, :-1], kb], axis=2)
    vw = jnp.concatenate([jnp.pad(vb, pad)[:, :-1], vb], axis=2)
    s = jnp.einsum('bnqhgd,bnkhd->bnhgqk', qb, kw).astype(jnp.float32) * scale
    qi = jnp.arange(BLOCK)[:, None]
    kj = jnp.arange(2 * BLOCK)[None, :]
    rel = qi + BLOCK - kj
    band = (rel >= 0) & (rel < WINDOW)
    exists = (jnp.arange(nb)[:, None, None] * BLOCK + kj[None]) >= BLOCK
    valid = band[None] & exists
    s = jnp.where(valid[None, :, None, None], s, -jnp.inf)
    sink = sinks.astype(jnp.float32).reshape(SWA_KV_HEADS, SWA_GROUP)[None, None, :, :, None, None]
    m = jnp.maximum(jnp.max(s, axis=-1, keepdims=True), sink)
    e = jnp.exp(s - m)
    p = e / (jnp.sum(e, axis=-1, keepdims=True) + jnp.exp(sink - m))
    o = jnp.einsum('bnhgqk,bnkhd->bnqhgd', p.astype(v.dtype), vw)
    return o.reshape(Bn, S, SWA_OUT)


def differential_attention(q, k, v, lam):
    Bn, S = q.shape[0], q.shape[1]
    nb = S // Q_BLOCK
    scale = HEAD_DIM ** -0.5
    qb = q.reshape(Bn, nb, Q_BLOCK, DIFF_HEADS, 2, HEAD_DIM).swapaxes(0, 1)
    starts = jnp.arange(nb) * Q_BLOCK
    kpos = jnp.arange(S)

    def one_block(args):
        qblk, start = args
        s = jnp.einsum('bqhcd,bkhcd->bhcqk', qblk, k).astype(jnp.float32) * scale
        mask = kpos[None, :] <= (start + jnp.arange(Q_BLOCK))[:, None]
        s = jnp.where(mask, s, -jnp.inf)
        p = jax.nn.softmax(s, axis=-1)
        a = p[:, :, 0] - lam * p[:, :, 1]
        return jnp.einsum('bhqk,bkhe->bqhe', a.astype(v.dtype), v)

    o = lax.map(one_block, (qb, starts))
    return o.swapaxes(0, 1).reshape(Bn, S, DIFF_HEADS, DIFF_VDIM)


def cross_attention(h, m, w_q, w_kv, w_o):
    Bn, S = h.shape[0], h.shape[1]
    q = (h @ w_q).reshape(Bn, S, CROSS_HEADS, CROSS_HEAD_DIM)
    kv = (m @ w_kv).reshape(Bn, m.shape[1], 2, CROSS_HEADS, CROSS_HEAD_DIM)
    k, v = kv[:, :, 0], kv[:, :, 1]
    s = jnp.einsum('bshd,bmhd->bhsm', q, k).astype(jnp.float32) * CROSS_HEAD_DIM ** -0.5
    p = jax.nn.softmax(s, axis=-1)
    o = jnp.einsum('bhsm,bmhd->bshd', p.astype(v.dtype), v).reshape(Bn, S, D_MODEL)
    return o @ w_o


def setup_inputs(seed: int = 0) -> dict:
    key = jax.random.key(seed)
    ks = jax.random.split(key, 24)
    f32 = jnp.float32
    L = DEPTH

    def w(k, shape, fan_in):
        return jax.random.normal(k, shape, f32) * fan_in ** -0.5

    def gain(k, shape):
        return 1.0 + 0.02 * jax.random.normal(k, shape, f32)

    x = jax.random.normal(ks[0], (BATCH, SEQ, D_MODEL), f32)
    mem = jax.random.normal(ks[1], (BATCH, MEM_LEN, D_MODEL), f32)
    positions = (jnp.arange(SEQ, dtype=jnp.int32)[None, :]
                 + jax.random.randint(ks[2], (BATCH, 1), 0, 1024, dtype=jnp.int32))
    return {
        "x": x,
        "mem": mem,
        "positions": positions,
        "g_mix": gain(ks[3], (L, D_MODEL)),
        "w_in": w(ks[4], (L, D_MODEL, IN_COLS), D_MODEL),
        "sinks": 0.5 * jax.random.normal(ks[5], (L, SWA_Q_HEADS), f32),
        "lambda_q1": 0.1 * jax.random.normal(ks[6], (L, HEAD_DIM), f32),
        "lambda_k1": 0.1 * jax.random.normal(ks[7], (L, HEAD_DIM), f32),
        "lambda_q2": 0.1 * jax.random.normal(ks[8], (L, HEAD_DIM), f32),
        "lambda_k2": 0.1 * jax.random.normal(ks[9], (L, HEAD_DIM), f32),
        "g_diff": gain(ks[10], (L, DIFF_VDIM)),
        "w_out": w(ks[11], (L, MIX_WIDTH, D_MODEL), MIX_WIDTH),
        "g_cross": gain(ks[12], (L, D_MODEL)),
        "g_mem": gain(ks[13], (L, D_MODEL)),
        "w_cq": w(ks[14], (L, D_MODEL, D_MODEL), D_MODEL),
        "w_ckv": w(ks[15], (L, D_MODEL, 2 * D_MODEL), D_MODEL),
        "w_co": w(ks[16], (L, D_MODEL, D_MODEL), D_MODEL),
        "g_mlp": gain(ks[17], (L, D_MODEL)),
        "w_up": w(ks[18], (L, D_MODEL, D_FF), D_MODEL),
        "w_down": w(ks[19], (L, D_FF, D_MODEL), D_FF),
        "g_final": gain(ks[20], (D_MODEL,)),
    }


def reference(x, mem, positions, g_mix, w_in, sinks, lambda_q1, lambda_k1, lambda_q2, lambda_k2,
              g_diff, w_out, g_cross, g_mem, w_cq, w_ckv, w_co, g_mlp, w_up, w_down, g_final):
    Bn, S = x.shape[0], x.shape[1]
    cos, sin = rope_tables(positions, HEAD_DIM)
    splits = np.cumsum([SWA_Q_COLS, SWA_KV_COLS, SWA_KV_COLS, DIFF_QK_COLS, DIFF_QK_COLS])
    for l in range(DEPTH):
        h = rms_norm(x, g_mix[l])
        proj = h @ w_in[l]
        qa, ka, va, qd, kd, vd = jnp.split(proj, splits, axis=-1)
        qa = apply_rope(qa.reshape(Bn, S, SWA_KV_HEADS, SWA_GROUP, HEAD_DIM), cos, sin)
        ka = apply_rope(ka.reshape(Bn, S, SWA_KV_HEADS, HEAD_DIM), cos, sin)
        va = va.reshape(Bn, S, SWA_KV_HEADS, HEAD_DIM)
        out_a = sliding_window_sink_attention(qa, ka, va, sinks[l])

        lam_init = 0.8 - 0.6 * math.exp(-0.3 * l)
        lam = (jnp.exp(jnp.sum(lambda_q1[l].astype(jnp.float32) * lambda_k1[l].astype(jnp.float32)))
               - jnp.exp(jnp.sum(lambda_q2[l].astype(jnp.float32) * lambda_k2[l].astype(jnp.float32)))
               + lam_init)
        qd = apply_rope(qd.reshape(Bn, S, DIFF_HEADS, 2, HEAD_DIM), cos, sin)
        kd = apply_rope(kd.reshape(Bn, S, DIFF_HEADS, 2, HEAD_DIM), cos, sin)
        vd = vd.reshape(Bn, S, DIFF_HEADS, DIFF_VDIM)
        od = differential_attention(qd, kd, vd, lam)
        out_b = (rms_norm(od, g_diff[l]) * (1.0 - lam_init)).reshape(Bn, S, DIFF_OUT)

        x = x + jnp.concatenate([out_a, out_b], axis=-1) @ w_out[l]

        x = x + cross_attention(rms_norm(x, g_cross[l]), rms_norm(mem, g_mem[l]),
                                w_cq[l], w_ckv[l], w_co[l])

        u = rms_norm(x, g_mlp[l]) @ w_up[l]
        x = x + jnp.square(jax.nn.relu(u)) @ w_down[l]
    return rms_norm(x, g_final)
```

```python
import contextlib
import numpy as np
import ml_dtypes
import concourse.bass as bass
import concourse.mybir as mybir
from concourse.bass_utils import run_bass_kernel_spmd

F32 = mybir.dt.float32
BF16 = mybir.dt.bfloat16
I32 = mybir.dt.int32
ALU = mybir.AluOpType
AF = mybir.ActivationFunctionType
AX = mybir.AxisListType

SEQ = 4096
DM = 1024
NT = SEQ // 128
EPS = 1e-5
DEBUG_O = False


class Sched:
    ENG = ("pe", "act", "dve", "pool", "sp")

    def __init__(self, nc, stack):
        self.nc = nc
        self.stack = stack
        self.ops = {e: [] for e in self.ENG}
        self.cnt = {e: 0 for e in self.ENG}
        self.sem = {e: stack.enter_context(nc.semaphore("prog_" + e)) for e in self.ENG}
        self.known = {e: {} for e in self.ENG}
        self.last_w = {}
        self.readers = {}
        self.dma_slots = {}

    def _deps(self, eng, reads, writes, xreads=()):
        deps = []
        for k in reads:
            ev = self.last_w.get(k)
            if ev is not None:
                deps.append(ev)
        for k in xreads:
            for ev in self.readers.get(k, ()):
                if ev[2] != eng:
                    deps.append(ev)
        for k in writes:
            ev = self.last_w.get(k)
            if ev is not None:
                deps.append(ev)
            for ev in self.readers.get(k, ()):
                if ev[2] == eng:
                    continue
                deps.append(ev)
        out = {}
        for (s, v, e) in deps:
            if e == eng and eng in ("pe", "sp"):
                continue
            key = s.name
            if self.known[eng].get(key, 0) >= v:
                continue
            if key not in out or out[key][1] < v:
                out[key] = (s, v)
        for key, (s, v) in out.items():
            self.known[eng][key] = v
        return list(out.values())

    def _commit(self, ev, reads, writes):
        for k in reads:
            self.readers.setdefault(k, []).append(ev)
        for k in writes:
            self.last_w[k] = ev
            self.readers[k] = []

    def op(self, eng, fn, reads=(), writes=(), inc=True):
        xr = ()
        if eng != "pe":
            xr = [k for k in reads if isinstance(k, tuple) and k[0] == "B" and k not in writes]
        waits = self._deps(eng, reads, writes, xr)
        if inc:
            self.cnt[eng] += 1
            ev = (self.sem[eng], self.cnt[eng], eng)
            self.ops[eng].append((waits, fn, (self.sem[eng], 1)))
        else:
            ev = (self.sem[eng], self.cnt[eng] + 1, eng)
            self.ops[eng].append((waits, fn, None))
        self._commit(ev, reads, writes)
        return ev

    def dma(self, queue, slot, fn, reads=(), writes=()):
        if slot not in self.dma_slots:
            s = self.stack.enter_context(self.nc.semaphore("dma_" + slot))
            self.dma_slots[slot] = [s, 0]
        waits = self._deps(queue, reads, writes)
        sl = self.dma_slots[slot]
        sl[1] += 16
        ev = (sl[0], sl[1], "dma")
        self.ops[queue].append((waits, fn, (sl[0], 16)))
        self._commit(ev, reads, writes)
        return ev

    def wait_events(self, eng, events):
        waits = []
        for (s, v, e) in events:
            if self.known[eng].get(s.name, 0) >= v:
                continue
            self.known[eng][s.name] = v
            waits.append((s, v))
        if waits:
            self.ops[eng].append((waits, None, None))

    def barrier(self, skip=("precast",)):
        evs = [(self.sem[e], self.cnt[e], e) for e in self.ENG if self.cnt[e] > 0]
        evs += [(sl[0], sl[1], "dma") for k, sl in self.dma_slots.items() if k not in skip]
        for e in self.ENG:
            self.wait_events(e, [ev for ev in evs if ev[2] != e])

    def emit(self):
        nc = self.nc
        with nc.Block() as block:
            def mk(e):
                def body(engobj):
                    for (waits, fn, inc) in self.ops[e]:
                        for (s, v) in waits:
                            engobj.wait_ge(s, v)
                        if fn is not None:
                            ins = fn(engobj)
                            if inc is not None:
                                ins.then_inc(inc[0], inc[1])
                return body
            block.tensor(mk("pe"))
            block.scalar(mk("act"))
            block.vector(mk("dve"))
            block.gpsimd(mk("pool"))
            block.sync(mk("sp"))
        self.ops = {e: [] for e in self.ENG}


def build_nc():
    nc = bass.Bass("TRN2", target_bir_lowering=False)

    def din(name, shape, dt=F32):
        return nc.dram_tensor(name, shape, dt, kind="ExternalInput").ap()

    x = din("x", [SEQ, DM])
    mem = din("mem", [256, DM])
    pos_d = din("pos", [128, NT], I32)
    invf_d = din("invf", [128, 32])
    ident_d = din("ident", [128, 128], BF16)
    maskL_d = din("maskL", [128, 128], BF16)
    maskSW_d = din("maskSW", [128, 256], BF16)
    g_mix = din("g_mix", [DM]); g_cross = din("g_cross", [DM]); g_mem = din("g_mem", [DM])
    g_mlp = din("g_mlp", [DM]); g_final = din("g_final", [DM]); g_diff = din("g_diff", [128])
    sinks_d = din("sinks", [8])
    lam_d = [din(n, [64]) for n in ("lq1", "lk1", "lq2", "lk2")]
    w_in = din("w_in", [DM, 2304]); w_out = din("w_out", [DM, DM]); w_cq = din("w_cq", [DM, DM])
    w_ckv = din("w_ckv", [DM, 2048]); w_co = din("w_co", [DM, DM])
    w_up = din("w_up", [DM, 4096]); w_down = din("w_down", [4096, DM])
    out = nc.dram_tensor("out", [SEQ, DM], F32, kind="ExternalOutput").ap()
    if DEBUG_O:
        dbg = nc.dram_tensor("dbg", [SEQ, DM], BF16, kind="ExternalOutput").ap()
    wup_s = nc.dram_tensor("wup_s", [DM, 4096], BF16).ap()
    wdn_s = nc.dram_tensor("wdn_s", [4096, DM], BF16).ap()

    with contextlib.ExitStack() as st:
        S = Sched(nc, st)

        def sbt(stack, name, shape, dt):
            return stack.enter_context(nc.sbuf_tensor("sb_" + name, shape, dt))

        PS = st.enter_context(nc.psum_tensor("PS", [128, 4096], F32))

        def bank(b, n=1):
            return PS[:, b * 512:(b + n) * 512]

        def bankbf(b):
            return PS[:, b * 512:(b + 1) * 512].bitcast(BF16)

        def BK(b):
            return ("B", b)

        def MM(o, lhsT, rhs, start, stop, R, W, skip=False, inc=True):
            S.op("pe", lambda e: e.matmul(o, lhsT, rhs, start=start, stop=stop, skip_group_check=skip), R, W,
                 inc=inc)

        def ACT(o, i, func, R, W, scale=1.0, bias=None, accum=None):
            kw = {}
            if bias is not None:
                kw["bias"] = bias
            if accum is not None:
                kw["accum_out"] = accum
            S.op("act", lambda e: e.activation(o, i, func, scale=scale, **kw), R, W)

        def TT(eng, o, a, b, op, R, W):
            S.op(eng, lambda e: e.tensor_tensor(o, a, b, op), R, W)

        def TS(eng, o, a, s1, s2, op0, op1, R, W):
            if s2 is None:
                S.op(eng, lambda e: e.tensor_scalar(o, a, s1, None, op0), R, W)
            else:
                S.op(eng, lambda e: e.tensor_scalar(o, a, s1, s2, op0, op1), R, W)

        def STT(eng, o, a, sc, b, op0, op1, R, W):
            S.op(eng, lambda e: e.scalar_tensor_tensor(o, a, sc, b, op0, op1), R, W)

        def CP(eng, o, a, R, W):
            if eng == "act":
                ACT(o, a, AF.Copy, R, W)
            else:
                S.op(eng, lambda e: e.tensor_copy(o, a), R, W)

        def RCP(o, a, R, W):
            S.op("dve", lambda e: e.reciprocal(o, a), R, W)

        def MEMSET(eng, ap, val, W):
            S.op(eng, lambda e: e.memset(ap, val), [], W)

        def DMA(q, slot, o, i, R, W):
            return S.dma(q, slot, lambda e: e.dma_start(out=o, in_=i), R, W)

        ident = sbt(st, "ident", [128, 128], BF16)
        maskL = sbt(st, "maskL", [128, 128], BF16)
        maskSW = sbt(st, "maskSW", [128, 2, 128], BF16)
        O = sbt(st, "O", [128, NT, DM], BF16)
        epsT = sbt(st, "epsT", [128, 1], F32)
        onesc = sbt(st, "onesc", [128, 2], BF16)
        es = sbt(st, "es", [128, 8], F32)
        neglam = sbt(st, "neglam", [128, 1], F32)
        gd_bc = sbt(st, "gd_bc", [128, 128], F32)
        junk = sbt(st, "junk", [128, DM], BF16)
        ssqs = sbt(st, "ssqs", [128, 64], F32)
        rss = sbt(st, "rss", [128, 64], F32)

        DMA("sp", "c0", ident[:], ident_d, [], ["ident"])
        DMA("sp", "c1", maskL[:], maskL_d, [], ["maskL"])
        DMA("sp", "c2", maskSW[:].rearrange("p a b -> p (a b)"), maskSW_d, [], ["maskSW"])
        MEMSET("dve", epsT[:], EPS, ["eps"])
        MEMSET("dve", onesc[:], 1.0, ["onesc"])
        MEMSET("dve", ssqs[:], 0.0, [("ssq", i) for i in range(64)])

        def RSTD(dst, ssq, n, R, W):
            ACT(dst, ssq, AF.Ln, list(R) + ["eps"], W, scale=1.0 / n, bias=epsT[:, 0:1])
            ACT(dst, dst, AF.Exp, W, W, scale=-0.5)

        norm_ctr = [0]

        def NORM_CAST(dst_bf, src, gbc, gkey, Rsrc, Wdst):
            i = norm_ctr[0] % 64
            norm_ctr[0] += 1
            ACT(junk[:], src, AF.Square, Rsrc, [("ssq", i)], accum=ssqs[:, i:i + 1])
            RSTD(rss[:, i:i + 1], ssqs[:, i:i + 1], DM, [("ssq", i)], [("rs", i)])
            STT("dve", dst_bf, src, rss[:, i:i + 1], gbc, ALU.mult, ALU.mult,
                list(Rsrc) + [("rs", i), gkey], Wdst)
            return i

        precast_jobs = []
        for i in range(8):
            precast_jobs.append((wup_s[i * 128:(i + 1) * 128, :], w_up[i * 128:(i + 1) * 128, :], "wup_s"))
        for i in range(8):
            precast_jobs.append((wdn_s[i * 512:(i + 1) * 512, :], w_down[i * 512:(i + 1) * 512, :], "wdn_s"))

        def issue_precast(n):
            for _ in range(n):
                if precast_jobs:
                    o_, i_, k_ = precast_jobs.pop(0)
                    DMA("pool", "precast", o_, i_, [], [k_])

        with contextlib.ExitStack() as sa:
            xT = sbt(sa, "xT", [128, 8, SEQ], BF16)
            QK = sbt(sa, "QK", [128, 3, SEQ], BF16)
            Vp = sbt(sa, "Vp", [128, NT, 130], BF16)
            Wsl = sbt(sa, "Wsl", [128, 8, 448], BF16)
            cosT = sbt(sa, "cosT", [128, NT, 32], F32)
            sinT = sbt(sa, "sinT", [128, NT, 32], F32)
            pass
            s0 = contextlib.ExitStack()
            posi = sbt(s0, "posi", [128, NT], I32)
            posf = sbt(s0, "posf", [128, NT], F32)
            invT = sbt(s0, "invT", [128, 32], F32)
            lq = sbt(s0, "lq", [128, 4, 64], F32)
            prod = sbt(s0, "prod", [128, 2, 64], F32)
            sums = sbt(s0, "sums", [128, 2], F32)
            sums2 = sbt(s0, "sums2", [128, 1], F32)
            w_in_v = w_in.rearrange("(c p) n -> p c n", p=128)
            passes = [("swa", 0), ("swa", 1), ("diff", 0), ("diff", 1), ("diff", 2), ("diff", 3)]
            def pass_cfg(pi):
                kind, idx = passes[pi]
                if kind == "swa":
                    g = idx
                    return [(0, 256, g * 256), (256, 64, 512 + g * 64), (320, 64, 512 + g * 64),
                            (384, 64, 640 + g * 64)]
                h = idx
                return [(0, 128, 768 + h * 128), (128, 128, 1280 + h * 128), (256, 128, 1792 + h * 128)]

            def load_wsl(pi):
                for (d0, n, s0) in pass_cfg(pi):
                    DMA("pool", "wsl", Wsl[:, :, d0:d0 + n], w_in_v[:, :, s0:s0 + n], [], ["Wsl"])

            MEMSET("pool", Vp[:], 1.0, [("Vp", t) for t in range(NT)])
            load_wsl(0)
            DMA("sp", "c3", posi[:], pos_d, [], ["posi"])
            DMA("sp", "c4", invT[:], invf_d, [], ["invT"])
            DMA("sp", "c6", gd_bc[:], g_diff.partition_broadcast(128), [], ["gd"])
            DMA("sp", "c7", es[:], sinks_d.partition_broadcast(128), [], ["es"])
            for i in range(4):
                DMA("sp", "c8", lq[:, i, :], lam_d[i].partition_broadcast(128), [], ["lq"])
            TS("dve", gd_bc[:], gd_bc[:], 0.8, None, ALU.mult, None, ["gd"], ["gd"])
            CP("dve", posf[:], posi[:], ["posi"], ["posf"])
            angt = sbt(s0, "angt", [128, NT * 32], F32)
            t_a = sbt(s0, "t_a", [128, NT * 32], F32)
            t_b = sbt(s0, "t_b", [128, NT * 32], F32)
            t_i = sbt(s0, "t_i", [128, NT * 32], I32)
            TT("dve", angt[:].rearrange("p (t i) -> p t i", i=32),
               posf[:].unsqueeze(2).to_broadcast([128, NT, 32]),
               invT[:].unsqueeze(1).to_broadcast([128, NT, 32]), ALU.mult, ["posf", "invT"], ["angt"])

            def sin_of(dst, shift):
                TS("dve", t_a[:], angt[:], float(shift), None, ALU.add, None, ["angt"], ["t_a"])
                TS("dve", t_b[:], t_a[:], float(1.0 / (2 * np.pi)), None, ALU.mult, None, ["t_a"], ["t_b"])
                CP("dve", t_i[:], t_b[:], ["t_b"], ["t_i"])
                CP("dve", t_b[:], t_i[:], ["t_i"], ["t_b"])
                STT("dve", t_a[:], t_b[:], float(-2 * np.pi), t_a[:], ALU.mult, ALU.add, ["t_b", "t_a"], ["t_a"])
                TS("dve", t_a[:], t_a[:], float(np.pi), float(-np.pi), ALU.min, ALU.max, ["t_a"], ["t_a"])
                ACT(dst, t_a[:], AF.Sin, ["t_a"], ["cs"])

            sin_of(sinT[:].rearrange("p t i -> p (t i)"), 0.0)
            sin_of(cosT[:].rearrange("p t i -> p (t i)"), np.pi / 2)
            TT("dve", prod[:, 0, :], lq[:, 0, :], lq[:, 1, :], ALU.mult, ["lq"], ["prod"])
            TT("dve", prod[:, 1, :], lq[:, 2, :], lq[:, 3, :], ALU.mult, ["lq", "prod"], ["prod"])
            S.op("dve", lambda e: e.reduce_sum(sums[:], prod[:], AX.X), ["prod"], ["sums"])
            ACT(sums[:], sums[:], AF.Exp, ["sums"], ["sums"])
            TT("dve", sums2[:], sums[:, 0:1], sums[:, 1:2], ALU.subtract, ["sums"], ["sums2"])
            TS("dve", neglam[:], sums2[:], -1.0, -0.2, ALU.mult, ALU.add, ["sums2"], ["neglam"])
            ACT(es[:], es[:], AF.Exp, ["es"], ["es"])

            S.barrier()
            S.emit()
            s0.close()
            pass
            s1 = contextlib.ExitStack()
            gmix_bc = sbt(s1, "gmix_bc", [128, DM], F32)
            xf = [sbt(s1, "xf%d" % i, [128, DM], F32) for i in range(4)]
            xb = [sbt(s1, "xb%d" % i, [128, DM], BF16) for i in range(2)]
            ssq0 = sbt(s1, "ssq0", [128, NT], F32)
            rstd0 = sbt(s1, "rstd0", [128, NT], F32)
            DMA("sp", "c5", gmix_bc[:], g_mix.partition_broadcast(128), [], ["gmix"])
            def a0_s1(t):
                b = t % 4
                DMA("sp", "x%d" % b, xf[b][:], x[t * 128:(t + 1) * 128, :], [], ["xf%d" % b])
                i = t
                ACT(junk[:], xf[b][:], AF.Square, ["xf%d" % b], [("ssq0", i)], accum=ssq0[:, i:i + 1])
                RSTD(rstd0[:, i:i + 1], ssq0[:, i:i + 1], DM, [("ssq0", i)], [("rs0", i)])

            def a0_s2(t):
                b = t % 2
                STT("dve", xb[b][:], xf[t % 4][:], rstd0[:, t:t + 1], gmix_bc[:], ALU.mult, ALU.mult,
                    ["xf%d" % (t % 4), ("rs0", t), "gmix"], ["xb%d" % b])
                tv = bankbf(b)
                for c in range(8):
                    S.op("pe", (lambda o, i: (lambda e: e.transpose(o, i, ident[:])))(
                        tv[:, c * 128:(c + 1) * 128], xb[b][:, c * 128:(c + 1) * 128]),
                        ["xb%d" % b, "ident"], [BK(b)], inc=(c == 7))

            def a0_s3(t):
                b = t % 2
                tv = bankbf(b)
                CP("dve", xT[:, :, t * 128:(t + 1) * 128], tv.rearrange("p (c k) -> p c k", k=128),
                   [BK(b)], [("xT", t)])

            a0_s1(0)
            a0_s1(1)
            a0_s1(2)
            a0_s2(0)
            for t in range(NT):
                if t + 3 < NT:
                    a0_s1(t + 3)
                if t + 1 < NT:
                    a0_s2(t + 1)
                a0_s3(t)
            S.barrier()
            S.emit()
            s1.close()
            ropeA = [sbt(sa, "ropeA%d" % i, [128, 384], F32) for i in range(2)]
            ropeB = [sbt(sa, "ropeB%d" % i, [128, 384], F32) for i in range(2)]
            ropeR = [sbt(sa, "ropeR%d" % i, [128, 384], BF16) for i in range(2)]
            pTb = [sbt(sa, "pT%d" % i, [128, 1024], BF16) for i in range(2)]
            den = [sbt(sa, "den%d" % i, [128, 4], F32) for i in range(2)]
            accsb = [sbt(sa, "accsb%d" % i, [128, 4, 260], F32) for i in range(2)]
            odb = [sbt(sa, "od%d" % i, [128, 4, 128], F32) for i in range(2)]
            t1b = sbt(sa, "t1b", [128, 128], F32)
            rrb = [sbt(sa, "rr%d" % i, [128, 4, 2], F32) for i in range(2)]
            msq = [sbt(sa, "msq%d" % i, [128, 4], F32) for i in range(2)]
            rnb = [sbt(sa, "rn%d" % i, [128, 4], F32) for i in range(2)]


            scnt = [0]
            for pi, (kind, idx) in enumerate(passes):
                if kind == "swa":
                    g = idx
                    segs = [(0, 256, g * 256), (256, 64, 512 + g * 64), (320, 64, 512 + g * 64),
                            (384, 64, 640 + g * 64)]
                    ncols, nrope, vd = 448, 384, 64
                    blks = [0, 1, 2]
                else:
                    h = idx
                    segs = [(0, 128, 768 + h * 128), (128, 128, 1280 + h * 128), (256, 128, 1792 + h * 128)]
                    ncols, nrope, vd = 384, 256, 128
                    blks = [0, 2]
                H = nrope // 64
                def proj_mm(t):
                    pb = 2 + t % 2
                    tok = slice(t * 128, (t + 1) * 128)
                    for c in range(8):
                        MM(bank(pb)[:, 0:ncols], xT[:, c, tok], Wsl[:, c, 0:ncols], c == 0, c == 7,
                           [("xT", t), "Wsl"], [BK(pb)], inc=(c == 7))

                def proj_post(t):
                    b = t % 2
                    pb = 2 + b
                    tok = slice(t * 128, (t + 1) * 128)
                    Pv = bank(pb)[:, 0:nrope].rearrange("p (h two i) -> p h two i", two=2, i=32)
                    av = ropeA[b][:, 0:nrope].rearrange("p (h two i) -> p h two i", two=2, i=32)
                    bv = ropeB[b][:, 0:nrope].rearrange("p (h two i) -> p h two i", two=2, i=32)
                    rv = ropeR[b][:, 0:nrope].rearrange("p (h two i) -> p h two i", two=2, i=32)
                    cosb = cosT[:, t, :].unsqueeze(1).unsqueeze(1).to_broadcast([128, H, 2, 32])
                    sinb = sinT[:, t, :].unsqueeze(1).to_broadcast([128, H, 32])
                    TT("dve", av, Pv, cosb, ALU.mult, [BK(pb), "cs"], ["ropeA%d" % b])
                    TT("dve", bv[:, :, 0, :], Pv[:, :, 1, :], sinb, ALU.mult, [BK(pb), "cs"], ["ropeB%d" % b])
                    TT("dve", bv[:, :, 1, :], Pv[:, :, 0, :], sinb, ALU.mult, [BK(pb), "cs"], ["ropeB%d" % b])
                    TT("pool", rv[:, :, 0, :], av[:, :, 0, :], bv[:, :, 0, :], ALU.subtract,
                       ["ropeA%d" % b, "ropeB%d" % b], ["ropeR%d" % b])
                    TT("pool", rv[:, :, 1, :], av[:, :, 1, :], bv[:, :, 1, :], ALU.add,
                       ["ropeA%d" % b, "ropeB%d" % b], ["ropeR%d" % b])
                    ACT(Vp[:, t, 0:vd], bank(pb)[:, nrope:nrope + vd], AF.Copy, [BK(pb)], [("Vp", t)])

                def proj_tr(t):
                    b = t % 2
                    tok = slice(t * 128, (t + 1) * 128)
                    tb = 4 + b
                    tv = bankbf(tb)
                    for i in range(len(blks)):
                        S.op("pe", (lambda o, ii: (lambda e: e.transpose(o, ii, ident[:])))(
                            tv[:, i * 128:(i + 1) * 128], ropeR[b][:, i * 128:(i + 1) * 128]),
                            ["ropeR%d" % b, "ident"], [BK(tb)], inc=(i == len(blks) - 1))
                    if kind == "swa":
                        ACT(QK[:, 0:3, tok], tv[:, 0:384].rearrange("p (c k) -> p c k", k=128), AF.Copy,
                            [BK(tb)], [("QK", 0, t), ("QK", 1, t), ("QK", 2, t)])
                    else:
                        ACT(QK[:, 0, tok], tv[:, 0:128], AF.Copy, [BK(tb)], [("QK", 0, t)])
                        CP("dve", QK[:, 2, tok], tv[:, 128:256], [BK(tb)], [("QK", 2, t)])

                proj_mm(0)
                proj_mm(1)
                proj_post(0)
                for t in range(NT):
                    if t + 2 < NT:
                        proj_mm(t + 2)
                    if t + 1 < NT:
                        proj_post(t + 1)
                    proj_tr(t)
                if pi + 1 < len(passes):
                    load_wsl(pi + 1)
                issue_precast(3)

                if kind == "swa":
                    def swa_qk(n):
                        sb_ = 0 if n % 2 == 0 else 6
                        psS = PS[:, sb_ * 512:(sb_ + 2) * 512].rearrange(
                            "p (par jj kb q) -> p par jj kb q", par=2, jj=2, kb=2)
                        kbs = [1] if n == 0 else [0, 1]
                        qtok = slice(n * 128, (n + 1) * 128)
                        for j in range(4):
                            par, jj = j % 2, j // 2
                            rows = slice(par * 64, par * 64 + 64)
                            for kb in kbs:
                                m = n - 1 + kb
                                MM(psS[:, par, jj, kb, :], QK[rows, 2, m * 128:(m + 1) * 128], QK[rows, jj, qtok],
                                   True, True, [("QK", 2, m), ("QK", jj, n)], [BK(sb_ + par)],
                                   inc=(j == 3 and kb == kbs[-1]))

                    def swa_rest(n):
                        sb_ = 0 if n % 2 == 0 else 6
                        psS = PS[:, sb_ * 512:(sb_ + 2) * 512].rearrange(
                            "p (par jj kb q) -> p par jj kb q", par=2, jj=2, kb=2)
                        kbs = [1] if n == 0 else [0, 1]
                        pk = "pT%d" % (n % 2)
                        pT = pTb[n % 2][:].rearrange("p (par jj kb q) -> p par jj kb q", par=2, jj=2, kb=2)
                        if n == 0:
                            for par in range(2):
                                ACT(pT[:, par, :, 1, :], psS[:, par, :, 1, :], AF.Exp, [BK(sb_ + par)], [pk],
                                    scale=0.125)
                                TT("dve", pT[:, par, :, 1, :], pT[:, par, :, 1, :],
                                   maskSW[:, 1, :].unsqueeze(1).to_broadcast([128, 2, 128]), ALU.mult,
                                   [pk, "maskSW"], [pk])
                        else:
                            ACT(pTb[n % 2][:], PS[:, sb_ * 512:(sb_ + 2) * 512], AF.Exp,
                                [BK(sb_), BK(sb_ + 1)], [pk], scale=0.125)
                            p4 = pTb[n % 2][:].rearrange("p (a kb q) -> p a kb q", a=4, kb=2)
                            TT("dve", p4, p4, maskSW[:].unsqueeze(1).to_broadcast([128, 4, 2, 128]), ALU.mult,
                               [pk, "maskSW"], [pk])
                        ab = 2 + n % 2
                        psA = bank(ab)[:, 0:264].rearrange("p (j d) -> p j d", d=66)
                        for j in range(4):
                            par, jj = j % 2, j // 2
                            for ii, kb in enumerate(kbs):
                                m = n - 1 + kb
                                MM(psA[:, j, 0:65], pT[:, par, jj, kb, :], Vp[:, m, 0:65], ii == 0,
                                   ii == len(kbs) - 1, [pk, ("Vp", m)], [BK(ab)], skip=True,
                                   inc=(j == 3 and ii == len(kbs) - 1))
                    def swa_norm(n):
                        ab = 2 + n % 2
                        psA = bank(ab)[:, 0:264].rearrange("p (j d) -> p j d", d=66)
                        dk = "den%d" % (n % 2)
                        dn = den[n % 2]
                        TT("dve", dn[:], psA[:, :, 64], es[:, 4 * g:4 * g + 4], ALU.add, [BK(ab), "es"], [dk])
                        RCP(dn[:], dn[:], [dk], [dk])
                        TT("dve", O[:, n, g * 256:(g + 1) * 256].rearrange("p (j d) -> p j d", d=64),
                           psA[:, :, 0:64], dn[:].unsqueeze(2).to_broadcast([128, 4, 64]), ALU.mult,
                           [BK(ab), dk], [("O", n)])

                    swa_qk(0)
                    for n in range(NT):
                        if n + 1 < NT:
                            swa_qk(n + 1)
                        swa_rest(n)
                        if n > 0:
                            swa_norm(n - 1)
                    swa_norm(NT - 1)
                else:
                    its = [(J, kb) for J in range(8) for kb in range(4 * J + 4)]
                    bufs = {}

                    def d_qk(i):
                        J, kb = its[i]
                        buf = scnt[0] % 2
                        scnt[0] += 1
                        bufs[i] = buf
                        sbk = 0 if buf == 0 else 2
                        psS = PS[:, sbk * 512:(sbk + 2) * 512].rearrange("p (m q) -> p m q", m=2)
                        v = kb - 4 * J
                        c0 = max(v, 0) * 128
                        for mp in range(2):
                            rows = slice(mp * 64, mp * 64 + 64)
                            MM(psS[:, mp, c0:512], QK[rows, 2, kb * 128:(kb + 1) * 128],
                               QK[rows, 0, J * 512 + c0:(J + 1) * 512], True, True,
                               [("QK", 2, kb)] + [("QK", 0, 4 * J + u) for u in range(c0 // 128, 4)],
                               [BK(sbk + mp)], inc=(mp == 1))

                    def d_rest(i):
                        J, kb = its[i]
                        buf = bufs[i]
                        sbk = 0 if buf == 0 else 2
                        psS = PS[:, sbk * 512:(sbk + 2) * 512].rearrange("p (m q) -> p m q", m=2)
                        v = kb - 4 * J
                        c0 = max(v, 0) * 128
                        pk = "pT%d" % buf
                        pT = pTb[buf][:].rearrange("p (m q) -> p m q", m=2)
                        ACT(pT[:, :, c0:512], psS[:, :, c0:512], AF.Exp, [BK(sbk), BK(sbk + 1)], [pk],
                            scale=0.125)
                        if v >= 0:
                            TT("dve", pT[:, :, c0:c0 + 128], pT[:, :, c0:c0 + 128],
                               maskL[:].unsqueeze(1).to_broadcast([128, 2, 128]), ALU.mult,
                               [pk, "maskL"], [pk])
                        for u in range(max(v, 0), 4):
                            psA = bank(4 + u)[:, 0:260].rearrange("p (m e) -> p m e", m=2)
                            for mp in range(2):
                                MM(psA[:, mp, 0:129], pT[:, mp, u * 128:(u + 1) * 128], Vp[:, kb, 0:129],
                                   kb == 0 and mp == 0, kb == 4 * J + u, [pk, ("Vp", kb)], [BK(4 + u)],
                                   skip=True, inc=(u == 3 and mp == 1))

                    def d_fin_copy(J):
                        jb = J % 2
                        ak = "accsb%d" % jb
                        CP("dve", accsb[jb][:].rearrange("p u (m e) -> p u m e", m=2)[:, :, :, 0:129],
                           PS[:, 2048:4096].rearrange("p (u c) -> p u c", u=4)[:, :, 0:260].rearrange(
                               "p u (m e) -> p u m e", m=2)[:, :, :, 0:129],
                           [BK(4), BK(5), BK(6), BK(7)], [ak])

                    def d_fin(J):
                        jb = J % 2
                        ak = "accsb%d" % jb
                        rk = "rr%d" % jb
                        av4 = accsb[jb][:].rearrange("p u (m e) -> p u m e", m=2)
                        RCP(rrb[jb][:], av4[:, :, :, 128], [ak], [rk])
                        TS("dve", rrb[jb][:, :, 1], rrb[jb][:, :, 1], neglam[:, 0:1], None, ALU.mult, None,
                           [rk, "neglam"], [rk])
                        ok_ = "od%d" % jb
                        for u in range(4):
                            TS("dve", t1b[:], av4[:, u, 0, 0:128], rrb[jb][:, u, 0:1], None, ALU.mult, None,
                               [ak, rk], ["t1b"])
                            STT("dve", odb[jb][:, u, :], av4[:, u, 1, 0:128], rrb[jb][:, u, 1:2], t1b[:],
                                ALU.mult, ALU.add, [ak, rk, "t1b"], [ok_])
                            ACT(junk[:, 0:128], odb[jb][:, u, :], AF.Square, [ok_], [("msq", jb, u)],
                                accum=msq[jb][:, u:u + 1])
                        RSTD(rnb[jb][:], msq[jb][:], 128, [("msq", jb, u) for u in range(4)], ["rn%d" % jb])
                        for u in range(4):
                            STT("dve", O[:, 4 * J + u, 512 + h * 128:512 + (h + 1) * 128], odb[jb][:, u, :],
                                rnb[jb][:, u:u + 1], gd_bc[:], ALU.mult, ALU.mult,
                                [ok_, "rn%d" % jb, "gd"], [("O", 4 * J + u)])

                    d_qk(0)
                    pending = []
                    for i in range(len(its)):
                        if i + 1 < len(its):
                            d_qk(i + 1)
                        d_rest(i)
                        if pending and pending[0][0] <= i:
                            d_fin(pending.pop(0)[1])
                        if its[i][1] == 4 * its[i][0] + 3:
                            d_fin_copy(its[i][0])
                            pending.append((i + 3, its[i][0]))
                    while pending:
                        d_fin(pending.pop(0)[1])
            if DEBUG_O:
                for t in range(NT):
                    ev = DMA("sp", "dbg", dbg[t * 128:(t + 1) * 128, :], O[:, t, :], [("O", t)], [])
            S.barrier()
            S.emit()

        with contextlib.ExitStack() as sbk_:
            Wo = sbt(sbk_, "Wo", [128, 8, DM], BF16)
            Wq = sbt(sbk_, "Wq", [128, 8, DM], BF16)
            Wc = sbt(sbk_, "Wc", [128, 8, DM], BF16)
            wupb = sbt(sbk_, "wupb", [128, 3, 8, 256], BF16)
            wdnb = sbt(sbk_, "wdnb", [128, 3, 2, DM], BF16)
            xr = [sbt(sbk_, "xr%d" % i, [128, DM], F32) for i in range(2)]
            gbcA = sbt(sbk_, "gbcA", [128, DM], F32)
            gbcB = sbt(sbk_, "gbcB", [128, DM], F32)
            gbcC = sbt(sbk_, "gbcC", [128, DM], F32)
            ot = sbt(sbk_, "ot", [128, DM], F32)
            xT12 = sbt(sbk_, "xT12", [128, 8, 256], BF16)
            memKT = sbt(sbk_, "memKT", [128, 8, 256], BF16)
            memV = sbt(sbk_, "memV", [128, 2, DM], BF16)
            qcT = sbt(sbk_, "qcT", [128, 8, 256], BF16)
            pcT = sbt(sbk_, "pcT", [128, 2048], BF16)
            OT2 = [sbt(sbk_, "OT%d" % i, [128, 8, 128], BF16) for i in range(2)]
            ocT2 = [sbt(sbk_, "ocT%d" % i, [128, 8, 128], BF16) for i in range(2)]
            ocn2 = [sbt(sbk_, "ocn%d" % i, [128, DM], BF16) for i in range(2)]
            xbB2 = [sbt(sbk_, "xbB%d" % i, [128, DM], BF16) for i in range(2)]
            xbB = xbB2[0]
            rl = [sbt(sbk_, "rl%d" % i, [128, 256], F32) for i in range(2)]
            hT = [sbt(sbk_, "hT%d" % i, [128, 2, 256], BF16) for i in range(2)]
            rD = sbt(sbk_, "rD", [128, 8], F32)

            def TRB(o, i, R, W, inc=True):
                S.op("pe", lambda e: e.transpose(o, i, ident[:]), list(R) + ["ident"], W, inc=inc)

            def wv(wd):
                return wd.rearrange("(c p) n -> p c n", p=128)

            DMA("pool", "wB", Wo[:], wv(w_out), [], ["Wo"])
            DMA("pool", "wB", Wq[:], wv(w_cq), [], ["Wq"])
            DMA("pool", "wB", Wc[:], wv(w_co), [], ["Wc"])
            DMA("sp", "g0", gbcA[:], g_mem.partition_broadcast(128), [], ["gbcA"])
            DMA("sp", "g1", gbcB[:], g_mlp.partition_broadcast(128), [], ["gbcB"])
            DMA("sp", "g2", gbcC[:], g_final.partition_broadcast(128), [], ["gbcC"])
            MEMSET("dve", ssqs[:], 0.0, [("ssq", i) for i in range(64)])

            wkv = wupb[:, 0:2, :, :].rearrange("p a c n -> p (a c n)").rearrange("p (c n) -> p c n", c=8)
            for mt in range(2):
                DMA("sp", "xB%d" % mt, xr[mt][:], mem[mt * 128:(mt + 1) * 128, :], [], ["xr%d" % mt])
                NORM_CAST(xbB[:], xr[mt][:], gbcA[:], "gbcA", ["xr%d" % mt], ["xbB0"])
                tv = bankbf(mt)
                for c in range(8):
                    TRB(tv[:, c * 128:(c + 1) * 128], xbB[:, c * 128:(c + 1) * 128], ["xbB0"], [BK(mt)], inc=(c == 7))
                ACT(xT12[:, :, mt * 128:(mt + 1) * 128], tv.rearrange("p (c k) -> p c k", k=128), AF.Copy,
                    [BK(mt)], [("xT12", 0), ("xT12", 1)])
            w_ckv_v = wv(w_ckv)
            for q4 in range(4):
                DMA("pool", "wB", wkv, w_ckv_v[:, :, q4 * 512:(q4 + 1) * 512], [], ["wup0", "wup1"])
                for mt in range(2):
                    pb = 2 + mt
                    for c in range(8):
                        MM(bank(pb), xT12[:, c, mt * 128:(mt + 1) * 128], wkv[:, c, :], c == 0, c == 7,
                           [("xT12", 0), ("xT12", 1), "wup0", "wup1"], [BK(pb)], inc=(c == 7))
                    if q4 < 2:
                        CP("dve", xbB[:, 0:512], bank(pb), [BK(pb)], ["xbB0"])
                        tv = bankbf(4 + mt)
                        for c in range(4):
                            TRB(tv[:, c * 128:(c + 1) * 128], xbB[:, c * 128:(c + 1) * 128], ["xbB0"], [BK(4 + mt)], inc=(c == 3))
                        ACT(memKT[:, q4 * 4:(q4 + 1) * 4, mt * 128:(mt + 1) * 128],
                            tv[:, 0:512].rearrange("p (c k) -> p c k", k=128), AF.Copy, [BK(4 + mt)], ["memKT"])
                    else:
                        ACT(memV[:, mt, (q4 - 2) * 512:(q4 - 1) * 512], bank(pb), AF.Copy, [BK(pb)], ["memV"])
            DMA("sp", "g0", gbcA[:], g_cross.partition_broadcast(128), [], ["gbcA"])

            wup_v = wup_s.rearrange("(c p) n -> p c n", p=128)
            NCH = 16
            total_chunks = 16 * NCH

            def load_chunk(k):
                fc = k % NCH
                bb = k % 3
                DMA("sp", "wu%d" % bb, wupb[:, bb, :, :], wup_v[:, :, fc * 256:(fc + 1) * 256],
                    ["wup_s"], ["wup%d" % bb])
                DMA("sp", "wd%d" % bb, wdnb[:, bb, :, :],
                    wdn_s[fc * 256:(fc + 1) * 256, :].rearrange("(f p) n -> p f n", p=128),
                    ["wdn_s"], ["wdn%d" % bb])

            load_chunk(0)
            load_chunk(1)
            kchunk = 0
            for gi in range(16):
                def b1_steps_a(tig):
                    t = 2 * gi + tig
                    X = xr[tig]
                    xk = "xr%d" % tig
                    pbase = 4 * tig
                    OTt, xbt = OT2[tig], xbB2[tig]
                    otk, xbk = "OT%d" % tig, "xbB%d" % tig
                    st_ = []

                    def s0():
                        DMA("sp", "xB%d" % tig, X[:], x[t * 128:(t + 1) * 128, :], [], [xk])
                        tv = bankbf(pbase)
                        for f in range(8):
                            TRB(tv[:, f * 128:(f + 1) * 128], O[:, t, f * 128:(f + 1) * 128], [("O", t)],
                                [BK(pbase)], inc=(f == 7))
                        ACT(OTt[:], tv.rearrange("p (c k) -> p c k", k=128), AF.Copy, [BK(pbase)], [otk])
                    st_.append(s0)

                    def s1():
                        for hf in range(2):
                            pb = pbase + 1 + hf
                            for f in range(8):
                                MM(bank(pb), OTt[:, f, :], Wo[:, f, hf * 512:(hf + 1) * 512], f == 0, f == 7,
                                   [otk, "Wo"], [BK(pb)], inc=(f == 7))
                            TT("dve", X[:, hf * 512:(hf + 1) * 512], bank(pb), X[:, hf * 512:(hf + 1) * 512],
                               ALU.add, [BK(pb), xk], [xk])
                    st_.append(s1)

                    def s2():
                        NORM_CAST(xbt[:], X[:], gbcA[:], "gbcA", [xk], [xbk])
                    st_.append(s2)

                    def s3():
                        tv = bankbf(pbase + 3)
                        for c in range(8):
                            TRB(tv[:, c * 128:(c + 1) * 128], xbt[:, c * 128:(c + 1) * 128], [xbk], [BK(pbase + 3)],
                                inc=(c == 7))
                        ACT(xT12[:, :, tig * 128:(tig + 1) * 128], tv.rearrange("p (c k) -> p c k", k=128),
                            AF.Copy, [BK(pbase + 3)], [("xT12", tig)])
                    st_.append(s3)
                    return st_

                sa0, sa1 = b1_steps_a(0), b1_steps_a(1)
                for i in range(len(sa0)):
                    sa0[i]()
                    sa1[i]()
                for j in range(8):
                    pb = j % 4
                    for c in range(8):
                        MM(bank(pb)[:, 0:256], Wq[:, c, j * 128:(j + 1) * 128], xT12[:, c, :], c == 0, c == 7,
                           ["Wq", ("xT12", 0), ("xT12", 1)], [BK(pb)], inc=(c == 7))
                    CP("act" if j % 2 == 0 else "dve", qcT[:, j, :], bank(pb)[:, 0:256], [BK(pb)], [("qcT", j)])
                psS = PS[:, 2048:4096].rearrange("p (h m q) -> p h m q", h=4, m=2)
                for hc in range(4):
                    for mt in range(2):
                        for dc in range(2):
                            MM(psS[:, hc, mt, :], memKT[:, hc * 2 + dc, mt * 128:(mt + 1) * 128],
                               qcT[:, hc * 2 + dc, :], dc == 0, dc == 1,
                               ["memKT", ("qcT", hc * 2), ("qcT", hc * 2 + 1)], [BK(4 + hc)], skip=True,
                               inc=(mt == 1 and dc == 1))
                    if hc % 2 == 1:
                        h0 = hc - 1
                        ACT(pcT[:, h0 * 512:(h0 + 2) * 512], PS[:, 2048 + h0 * 512:2048 + (h0 + 2) * 512], AF.Exp,
                            [BK(4 + h0), BK(5 + h0)], [("pcT", h0 // 2)], scale=1.0 / 16)
                pc4 = pcT[:].rearrange("p (h m q) -> p h m q", h=4, m=2)

                def b1_steps_b(tig):
                    t = 2 * gi + tig
                    X = xr[tig]
                    xk = "xr%d" % tig
                    ocnt, ocTt, xbt = ocn2[tig], ocT2[tig], xbB2[tig]
                    ock, octk, xbk = "ocn%d" % tig, "ocT%d" % tig, "xbB%d" % tig
                    ob = 2 * tig
                    psO = PS[:, ob * 512:(ob + 2) * 512].rearrange("p (h e) -> p h e", h=4)
                    db = 4 + tig
                    psD = bank(db)
                    trb = 6 + tig
                    st_ = []

                    def s0():
                        for hc in range(4):
                            for mt in range(2):
                                MM(psO[:, hc, :], pc4[:, hc, mt, tig * 128:(tig + 1) * 128],
                                   memV[:, mt, hc * 256:(hc + 1) * 256], mt == 0, mt == 1,
                                   [("pcT", hc // 2), "memV"], [BK(ob + hc // 2)], skip=True, inc=False)
                            for mt in range(2):
                                MM(psD[:, hc:hc + 1], pc4[:, hc, mt, tig * 128:(tig + 1) * 128],
                                   onesc[:, 0:1], mt == 0, mt == 1, [("pcT", hc // 2), "onesc"], [BK(db)],
                                   skip=True, inc=(mt == 1))
                    st_.append(s0)

                    def s1():
                        RCP(rD[:, tig * 4:tig * 4 + 4], psD[:, 0:4], [BK(db)], [("rD", tig)])
                        TT("dve", ocnt[:].rearrange("p (h e) -> p h e", h=4), psO,
                           rD[:, tig * 4:tig * 4 + 4].unsqueeze(2).to_broadcast([128, 4, 256]), ALU.mult,
                           [BK(ob), BK(ob + 1), ("rD", tig)], [ock])
                    st_.append(s1)

                    def s2():
                        tv = bankbf(trb)
                        for c in range(8):
                            TRB(tv[:, c * 128:(c + 1) * 128], ocnt[:, c * 128:(c + 1) * 128], [ock], [BK(trb)],
                                inc=(c == 7))
                        ACT(ocTt[:], tv.rearrange("p (c k) -> p c k", k=128), AF.Copy, [BK(trb)], [octk])
                    st_.append(s2)

                    def s3():
                        for hf in range(2):
                            pb = ob + hf
                            for f in range(8):
                                MM(bank(pb), ocTt[:, f, :], Wc[:, f, hf * 512:(hf + 1) * 512], f == 0, f == 7,
                                   [octk, "Wc"], [BK(pb)], inc=(f == 7))
                            TT("dve", X[:, hf * 512:(hf + 1) * 512], bank(pb), X[:, hf * 512:(hf + 1) * 512],
                               ALU.add, [BK(pb), xk], [xk])
                    st_.append(s3)

                    def s4():
                        NORM_CAST(xbt[:], X[:], gbcB[:], "gbcB", [xk], [xbk])
                    st_.append(s4)

                    def s5():
                        tv = bankbf(trb)
                        for c in range(8):
                            TRB(tv[:, c * 128:(c + 1) * 128], xbt[:, c * 128:(c + 1) * 128], [xbk], [BK(trb)],
                                inc=(c == 7))
                        ACT(xT12[:, :, tig * 128:(tig + 1) * 128], tv.rearrange("p (c k) -> p c k", k=128),
                            AF.Copy, [BK(trb)], [("xT12", tig)])
                    st_.append(s5)
                    return st_

                sb0, sb1 = b1_steps_b(0), b1_steps_b(1)
                for i in range(len(sb0)):
                    sb0[i]()
                    sb1[i]()
                def mlp_up(k):
                    bb = k % 3
                    ubase = 0 if k % 2 == 0 else 2
                    for fs in range(2):
                        ub = ubase + fs
                        for c in range(8):
                            MM(bank(ub)[:, 0:256], wupb[:, bb, c, fs * 128:(fs + 1) * 128], xT12[:, c, :],
                               c == 0, c == 7, ["wup%d" % bb, ("xT12", 0), ("xT12", 1)], [BK(ub)], inc=(c == 7))

                def mlp_act(k):
                    bb = k % 3
                    hb = k % 2
                    ubase = 0 if k % 2 == 0 else 2
                    for fs in range(2):
                        ub = ubase + fs
                        ACT(rl[fs][:], bank(ub)[:, 0:256], AF.Relu, [BK(ub)], ["rl%d" % fs])
                        TT("pool" if fs == 0 else "dve", hT[hb][:, fs, :], rl[fs][:], rl[fs][:], ALU.mult,
                           ["rl%d" % fs], ["hT%d" % hb])

                def mlp_down(k, fc):
                    bb = k % 3
                    hb = k % 2
                    for fs in range(2):
                        for tig in range(2):
                            for hf in range(2):
                                yb = 4 + tig * 2 + hf
                                MM(bank(yb), hT[hb][:, fs, tig * 128:(tig + 1) * 128],
                                   wdnb[:, bb, fs, hf * 512:(hf + 1) * 512],
                                   fc == 0 and fs == 0, fc == NCH - 1 and fs == 1,
                                   ["hT%d" % hb, "wdn%d" % bb], [BK(yb)], inc=(fs == 1 and tig == 1 and hf == 1))

                k0 = kchunk
                mlp_up(k0)
                for fc in range(NCH):
                    k = k0 + fc
                    if k + 2 < total_chunks:
                        load_chunk(k + 2)
                    if fc + 1 < NCH:
                        mlp_up(k + 1)
                    mlp_act(k)
                    mlp_down(k, fc)
                kchunk = k0 + NCH
                for tig in range(2):
                    t = 2 * gi + tig
                    X = xr[tig]
                    xk = "xr%d" % tig
                    for hf in range(2):
                        yb = 4 + tig * 2 + hf
                        TT("dve", X[:, hf * 512:(hf + 1) * 512], bank(yb), X[:, hf * 512:(hf + 1) * 512],
                           ALU.add, [BK(yb), xk], [xk])
                    NORM_CAST_F = norm_ctr[0] % 64
                    norm_ctr[0] += 1
                    i = NORM_CAST_F
                    ACT(junk[:], X[:], AF.Square, [xk], [("ssq", i)], accum=ssqs[:, i:i + 1])
                    RSTD(rss[:, i:i + 1], ssqs[:, i:i + 1], DM, [("ssq", i)], [("rs", i)])
                    STT("dve", ot[:], X[:], rss[:, i:i + 1], gbcC[:], ALU.mult, ALU.mult,
                        [xk, ("rs", i), "gbcC"], ["ot"])
                    DMA("sp", "out", out[t * 128:(t + 1) * 128, :], ot[:], ["ot"], [])
            S.barrier()
            S.emit()
    return nc


_CONST = None


def _consts():
    global _CONST
    if _CONST is None:
        bf = ml_dtypes.bfloat16
        k = np.arange(128)[:, None]
        q = np.arange(128)[None, :]
        ident = (k == q).astype(np.float32).astype(bf)
        maskL = (k <= q).astype(np.float32).astype(bf)
        maskU = (k > q).astype(np.float32).astype(bf)
        maskSW = np.concatenate([maskU, maskL], axis=1)
        invf = (10000.0 ** (-np.arange(0, 64, 2, dtype=np.float64) / 64.0)).astype(np.float32)
        invf = np.ascontiguousarray(np.broadcast_to(invf[None, :], (128, 32)))
        _CONST = dict(ident=ident, maskL=maskL, maskSW=np.ascontiguousarray(maskSW), invf=invf)
    return _CONST


def kernel(x, mem, positions, g_mix, w_in, sinks, lambda_q1, lambda_k1, lambda_q2, lambda_k2,
           g_diff, w_out, g_cross, g_mem, w_cq, w_ckv, w_co, g_mlp, w_up, w_down, g_final):
    f = lambda a: np.ascontiguousarray(np.asarray(a, dtype=np.float32))
    x = f(x); mem = f(mem)
    positions = np.asarray(positions).astype(np.int32)
    shared = dict(_consts())
    shared.update(
        g_mix=f(g_mix)[0], g_cross=f(g_cross)[0], g_mem=f(g_mem)[0], g_mlp=f(g_mlp)[0], g_final=f(g_final),
        g_diff=f(g_diff)[0], sinks=f(sinks)[0], lq1=f(lambda_q1)[0], lk1=f(lambda_k1)[0],
        lq2=f(lambda_q2)[0], lk2=f(lambda_k2)[0],
        w_in=f(w_in)[0], w_out=f(w_out)[0], w_cq=f(w_cq)[0], w_ckv=f(w_ckv)[0], w_co=f(w_co)[0],
        w_up=f(w_up)[0], w_down=f(w_down)[0])
    in_maps = []
    for b in range(8):
        m = dict(shared)
        m["x"] = x[b]
        m["mem"] = mem[b]
        m["pos"] = np.ascontiguousarray(positions[b].reshape(NT, 128).T)
        in_maps.append(m)
    nc = build_nc()
    res = run_bass_kernel_spmd(nc, in_maps, core_ids=list(range(8)))
    outp = np.stack([np.asarray(res.results[b]["out"], dtype=np.float32) for b in range(8)], axis=0)
    if DEBUG_O:
        kernel.dbg = np.stack([np.asarray(res.results[b]["dbg"]).astype(np.float32) for b in range(8)], axis=0)
    return outp
```

```python
import contextlib
import numpy as np
import ml_dtypes
import concourse.bass as bass
import concourse.mybir as mybir
from concourse.bass_utils import run_bass_kernel_spmd

F32 = mybir.dt.float32
BF16 = mybir.dt.bfloat16
I32 = mybir.dt.int32
ALU = mybir.AluOpType
AF = mybir.ActivationFunctionType
AX = mybir.AxisListType

SEQ = 4096
DM = 1024
NT = SEQ // 128
EPS = 1e-5
DEBUG_O = False


class Sched:
    ENG = ("pe", "act", "dve", "pool", "sp")

    def __init__(self, nc, stack):
        self.nc = nc
        self.stack = stack
        self.ops = {e: [] for e in self.ENG}
        self.cnt = {e: 0 for e in self.ENG}
        self.sem = {e: stack.enter_context(nc.semaphore("prog_" + e)) for e in self.ENG}
        self.known = {e: {} for e in self.ENG}
        self.last_w = {}
        self.readers = {}
        self.dma_slots = {}

    def _deps(self, eng, reads, writes, xreads=()):
        deps = []
        for k in reads:
            ev = self.last_w.get(k)
            if ev is not None:
                deps.append(ev)
        for k in xreads:
            for ev in self.readers.get(k, ()):
                if ev[2] != eng:
                    deps.append(ev)
        for k in writes:
            ev = self.last_w.get(k)
            if ev is not None:
                deps.append(ev)
            for ev in self.readers.get(k, ()):
                if ev[2] == eng and eng in ("pe", "sp"):
                    continue
                deps.append(ev)
        out = {}
        for (s, v, e) in deps:
            if e == eng and eng in ("pe", "sp"):
                continue
            key = s.name
            if self.known[eng].get(key, 0) >= v:
                continue
            if key not in out or out[key][1] < v:
                out[key] = (s, v)
        for key, (s, v) in out.items():
            self.known[eng][key] = v
        return list(out.values())

    def _commit(self, ev, reads, writes):
        for k in reads:
            self.readers.setdefault(k, []).append(ev)
        for k in writes:
            self.last_w[k] = ev
            self.readers[k] = []

    def op(self, eng, fn, reads=(), writes=(), inc=True):
        xr = ()
        if eng != "pe":
            xr = [k for k in reads if isinstance(k, tuple) and k[0] == "B" and k not in writes]
        waits = self._deps(eng, reads, writes, xr)
        if inc:
            self.cnt[eng] += 1
            ev = (self.sem[eng], self.cnt[eng], eng)
            self.ops[eng].append((waits, fn, (self.sem[eng], 1)))
        else:
            ev = (self.sem[eng], self.cnt[eng] + 1, eng)
            self.ops[eng].append((waits, fn, None))
        self._commit(ev, reads, writes)
        return ev

    def dma(self, queue, slot, fn, reads=(), writes=()):
        if slot not in self.dma_slots:
            s = self.stack.enter_context(self.nc.semaphore("dma_" + slot))
            self.dma_slots[slot] = [s, 0]
        waits = self._deps(queue, reads, writes)
        sl = self.dma_slots[slot]
        sl[1] += 16
        ev = (sl[0], sl[1], "dma")
        self.ops[queue].append((waits, fn, (sl[0], 16)))
        self._commit(ev, reads, writes)
        return ev

    def wait_events(self, eng, events):
        waits = []
        for (s, v, e) in events:
            if self.known[eng].get(s.name, 0) >= v:
                continue
            self.known[eng][s.name] = v
            waits.append((s, v))
        if waits:
            self.ops[eng].append((waits, None, None))

    def barrier(self, skip=("precast",)):
        evs = [(self.sem[e], self.cnt[e], e) for e in self.ENG if self.cnt[e] > 0]
        evs += [(sl[0], sl[1], "dma") for k, sl in self.dma_slots.items() if k not in skip]
        for e in self.ENG:
            self.wait_events(e, [ev for ev in evs if ev[2] != e])

    def emit(self):
        nc = self.nc
        with nc.Block() as block:
            def mk(e):
                def body(engobj):
                    for (waits, fn, inc) in self.ops[e]:
                        for (s, v) in waits:
                            engobj.wait_ge(s, v)
                        if fn is not None:
                            ins = fn(engobj)
                            if inc is not None:
                                ins.then_inc(inc[0], inc[1])
                return body
            block.tensor(mk("pe"))
            block.scalar(mk("act"))
            block.vector(mk("dve"))
            block.gpsimd(mk("pool"))
            block.sync(mk("sp"))
        self.ops = {e: [] for e in self.ENG}


def build_nc():
    nc = bass.Bass("TRN2", target_bir_lowering=False)

    def din(name, shape, dt=F32):
        return nc.dram_tensor(name, shape, dt, kind="ExternalInput").ap()

    x = din("x", [SEQ, DM])
    mem = din("mem", [256, DM])
    pos_d = din("pos", [128, NT], I32)
    invf_d = din("invf", [128, 32])
    ident_d = din("ident", [128, 128], BF16)
    maskL_d = din("maskL", [128, 128], BF16)
    maskSW_d = din("maskSW", [128, 256], BF16)
    g_mix = din("g_mix", [DM]); g_cross = din("g_cross", [DM]); g_mem = din("g_mem", [DM])
    g_mlp = din("g_mlp", [DM]); g_final = din("g_final", [DM]); g_diff = din("g_diff", [128])
    sinks_d = din("sinks", [8])
    lam_d = [din(n, [64]) for n in ("lq1", "lk1", "lq2", "lk2")]
    w_in = din("w_in", [DM, 2304]); w_out = din("w_out", [DM, DM]); w_cq = din("w_cq", [DM, DM])
    w_ckv = din("w_ckv", [DM, 2048]); w_co = din("w_co", [DM, DM])
    w_up = din("w_up", [DM, 4096]); w_down = din("w_down", [4096, DM])
    out = nc.dram_tensor("out", [SEQ, DM], F32, kind="ExternalOutput").ap()
    if DEBUG_O:
        dbg = nc.dram_tensor("dbg", [SEQ, DM], BF16, kind="ExternalOutput").ap()
    wup_s = nc.dram_tensor("wup_s", [DM, 4096], BF16).ap()
    wdn_s = nc.dram_tensor("wdn_s", [4096, DM], BF16).ap()

    with contextlib.ExitStack() as st:
        S = Sched(nc, st)

        def sbt(stack, name, shape, dt):
            return stack.enter_context(nc.sbuf_tensor("sb_" + name, shape, dt))

        PS = st.enter_context(nc.psum_tensor("PS", [128, 4096], F32))

        def bank(b, n=1):
            return PS[:, b * 512:(b + n) * 512]

        def bankbf(b):
            return PS[:, b * 512:(b + 1) * 512].bitcast(BF16)

        def BK(b):
            return ("B", b)

        def MM(o, lhsT, rhs, start, stop, R, W, skip=False, inc=True):
            S.op("pe", lambda e: e.matmul(o, lhsT, rhs, start=start, stop=stop, skip_group_check=skip), R, W,
                 inc=inc)

        def ACT(o, i, func, R, W, scale=1.0, bias=None, accum=None):
            kw = {}
            if bias is not None:
                kw["bias"] = bias
            if accum is not None:
                kw["accum_out"] = accum
            S.op("act", lambda e: e.activation(o, i, func, scale=scale, **kw), R, W)

        def TT(eng, o, a, b, op, R, W):
            S.op(eng, lambda e: e.tensor_tensor(o, a, b, op), R, W)

        def TS(eng, o, a, s1, s2, op0, op1, R, W):
            if s2 is None:
                S.op(eng, lambda e: e.tensor_scalar(o, a, s1, None, op0), R, W)
            else:
                S.op(eng, lambda e: e.tensor_scalar(o, a, s1, s2, op0, op1), R, W)

        def STT(eng, o, a, sc, b, op0, op1, R, W):
            S.op(eng, lambda e: e.scalar_tensor_tensor(o, a, sc, b, op0, op1), R, W)

        def CP(eng, o, a, R, W):
            if eng == "act":
                ACT(o, a, AF.Copy, R, W)
            else:
                S.op(eng, lambda e: e.tensor_copy(o, a), R, W)

        def RCP(o, a, R, W):
            S.op("dve", lambda e: e.reciprocal(o, a), R, W)

        def MEMSET(eng, ap, val, W):
            S.op(eng, lambda e: e.memset(ap, val), [], W)

        def DMA(q, slot, o, i, R, W):
            return S.dma(q, slot, lambda e: e.dma_start(out=o, in_=i), R, W)

        ident = sbt(st, "ident", [128, 128], BF16)
        maskL = sbt(st, "maskL", [128, 128], BF16)
        maskSW = sbt(st, "maskSW", [128, 2, 128], BF16)
        O = sbt(st, "O", [128, NT, DM], BF16)
        epsT = sbt(st, "epsT", [128, 1], F32)
        onesc = sbt(st, "onesc", [128, 2], BF16)
        es = sbt(st, "es", [128, 8], F32)
        neglam = sbt(st, "neglam", [128, 1], F32)
        gd_bc = sbt(st, "gd_bc", [128, 128], F32)
        junk = sbt(st, "junk", [128, DM], BF16)
        ssqs = sbt(st, "ssqs", [128, 64], F32)
        rss = sbt(st, "rss", [128, 64], F32)

        DMA("sp", "c0", ident[:], ident_d, [], ["ident"])
        DMA("sp", "c1", maskL[:], maskL_d, [], ["maskL"])
        DMA("sp", "c2", maskSW[:].rearrange("p a b -> p (a b)"), maskSW_d, [], ["maskSW"])
        MEMSET("dve", epsT[:], EPS, ["eps"])
        MEMSET("dve", onesc[:], 1.0, ["onesc"])
        MEMSET("dve", ssqs[:], 0.0, [("ssq", i) for i in range(64)])

        def RSTD(dst, ssq, n, R, W):
            ACT(dst, ssq, AF.Ln, list(R) + ["eps"], W, scale=1.0 / n, bias=epsT[:, 0:1])
            ACT(dst, dst, AF.Exp, W, W, scale=-0.5)

        norm_ctr = [0]

        def NORM_CAST(dst_bf, src, gbc, gkey, Rsrc, Wdst):
            i = norm_ctr[0] % 64
            norm_ctr[0] += 1
            ACT(junk[:], src, AF.Square, Rsrc, [("ssq", i)], accum=ssqs[:, i:i + 1])
            RSTD(rss[:, i:i + 1], ssqs[:, i:i + 1], DM, [("ssq", i)], [("rs", i)])
            STT("dve", dst_bf, src, rss[:, i:i + 1], gbc, ALU.mult, ALU.mult,
                list(Rsrc) + [("rs", i), gkey], Wdst)
            return i

        precast_jobs = []
        for i in range(8):
            precast_jobs.append((wup_s[i * 128:(i + 1) * 128, :], w_up[i * 128:(i + 1) * 128, :], "wup_s"))
        for i in range(8):
            precast_jobs.append((wdn_s[i * 512:(i + 1) * 512, :], w_down[i * 512:(i + 1) * 512, :], "wdn_s"))

        def issue_precast(n):
            for _ in range(n):
                if precast_jobs:
                    o_, i_, k_ = precast_jobs.pop(0)
                    DMA("pool", "precast", o_, i_, [], [k_])

        with contextlib.ExitStack() as sa:
            xT = sbt(sa, "xT", [128, 8, SEQ], BF16)
            QK = sbt(sa, "QK", [128, 3, SEQ], BF16)
            Vp = sbt(sa, "Vp", [128, NT, 130], BF16)
            Wsl = sbt(sa, "Wsl", [128, 8, 448], BF16)
            cosT = sbt(sa, "cosT", [128, NT, 32], F32)
            sinT = sbt(sa, "sinT", [128, NT, 32], F32)
            pass
            s0 = contextlib.ExitStack()
            posi = sbt(s0, "posi", [128, NT], I32)
            posf = sbt(s0, "posf", [128, NT], F32)
            invT = sbt(s0, "invT", [128, 32], F32)
            lq = sbt(s0, "lq", [128, 4, 64], F32)
            prod = sbt(s0, "prod", [128, 2, 64], F32)
            sums = sbt(s0, "sums", [128, 2], F32)
            sums2 = sbt(s0, "sums2", [128, 1], F32)
            w_in_v = w_in.rearrange("(c p) n -> p c n", p=128)
            passes = [("swa", 0), ("swa", 1), ("diff", 0), ("diff", 1), ("diff", 2), ("diff", 3)]
            def pass_cfg(pi):
                kind, idx = passes[pi]
                if kind == "swa":
                    g = idx
                    return [(0, 256, g * 256), (256, 64, 512 + g * 64), (320, 64, 512 + g * 64),
                            (384, 64, 640 + g * 64)]
                h = idx
                return [(0, 128, 768 + h * 128), (128, 128, 1280 + h * 128), (256, 128, 1792 + h * 128)]

            def load_wsl(pi):
                for (d0, n, s0) in pass_cfg(pi):
                    DMA("pool", "wsl", Wsl[:, :, d0:d0 + n], w_in_v[:, :, s0:s0 + n], [], ["Wsl"])

            MEMSET("pool", Vp[:], 1.0, [("Vp", t) for t in range(NT)])
            load_wsl(0)
            DMA("sp", "c3", posi[:], pos_d, [], ["posi"])
            DMA("sp", "c4", invT[:], invf_d, [], ["invT"])
            DMA("sp", "c6", gd_bc[:], g_diff.partition_broadcast(128), [], ["gd"])
            DMA("sp", "c7", es[:], sinks_d.partition_broadcast(128), [], ["es"])
            for i in range(4):
                DMA("sp", "c8", lq[:, i, :], lam_d[i].partition_broadcast(128), [], ["lq"])
            TS("dve", gd_bc[:], gd_bc[:], 0.8, None, ALU.mult, None, ["gd"], ["gd"])
            CP("dve", posf[:], posi[:], ["posi"], ["posf"])
            angt = sbt(s0, "angt", [128, NT * 32], F32)
            t_a = sbt(s0, "t_a", [128, NT * 32], F32)
            t_b = sbt(s0, "t_b", [128, NT * 32], F32)
            t_i = sbt(s0, "t_i", [128, NT * 32], I32)
            TT("dve", angt[:].rearrange("p (t i) -> p t i", i=32),
               posf[:].unsqueeze(2).to_broadcast([128, NT, 32]),
               invT[:].unsqueeze(1).to_broadcast([128, NT, 32]), ALU.mult, ["posf", "invT"], ["angt"])

            def sin_of(dst, shift):
                TS("dve", t_a[:], angt[:], float(shift), None, ALU.add, None, ["angt"], ["t_a"])
                TS("dve", t_b[:], t_a[:], float(1.0 / (2 * np.pi)), None, ALU.mult, None, ["t_a"], ["t_b"])
                CP("dve", t_i[:], t_b[:], ["t_b"], ["t_i"])
                CP("dve", t_b[:], t_i[:], ["t_i"], ["t_b"])
                STT("dve", t_a[:], t_b[:], float(-2 * np.pi), t_a[:], ALU.mult, ALU.add, ["t_b", "t_a"], ["t_a"])
                TS("dve", t_a[:], t_a[:], float(np.pi), float(-np.pi), ALU.min, ALU.max, ["t_a"], ["t_a"])
                ACT(dst, t_a[:], AF.Sin, ["t_a"], ["cs"])

            sin_of(sinT[:].rearrange("p t i -> p (t i)"), 0.0)
            sin_of(cosT[:].rearrange("p t i -> p (t i)"), np.pi / 2)
            TT("dve", prod[:, 0, :], lq[:, 0, :], lq[:, 1, :], ALU.mult, ["lq"], ["prod"])
            TT("dve", prod[:, 1, :], lq[:, 2, :], lq[:, 3, :], ALU.mult, ["lq", "prod"], ["prod"])
            S.op("dve", lambda e: e.reduce_sum(sums[:], prod[:], AX.X), ["prod"], ["sums"])
            ACT(sums[:], sums[:], AF.Exp, ["sums"], ["sums"])
            TT("dve", sums2[:], sums[:, 0:1], sums[:, 1:2], ALU.subtract, ["sums"], ["sums2"])
            TS("dve", neglam[:], sums2[:], -1.0, -0.2, ALU.mult, ALU.add, ["sums2"], ["neglam"])
            ACT(es[:], es[:], AF.Exp, ["es"], ["es"])

            S.barrier()
            S.emit()
            s0.close()
            pass
            s1 = contextlib.ExitStack()
            gmix_bc = sbt(s1, "gmix_bc", [128, DM], F32)
            xf = [sbt(s1, "xf%d" % i, [128, DM], F32) for i in range(4)]
            xb = [sbt(s1, "xb%d" % i, [128, DM], BF16) for i in range(2)]
            ssq0 = sbt(s1, "ssq0", [128, NT], F32)
            rstd0 = sbt(s1, "rstd0", [128, NT], F32)
            DMA("sp", "c5", gmix_bc[:], g_mix.partition_broadcast(128), [], ["gmix"])
            def a0_s1(t):
                b = t % 4
                DMA("sp", "x%d" % b, xf[b][:], x[t * 128:(t + 1) * 128, :], [], ["xf%d" % b])
                i = t
                ACT(junk[:], xf[b][:], AF.Square, ["xf%d" % b], [("ssq0", i)], accum=ssq0[:, i:i + 1])
                RSTD(rstd0[:, i:i + 1], ssq0[:, i:i + 1], DM, [("ssq0", i)], [("rs0", i)])

            def a0_s2(t):
                b = t % 2
                STT("dve", xb[b][:], xf[t % 4][:], rstd0[:, t:t + 1], gmix_bc[:], ALU.mult, ALU.mult,
                    ["xf%d" % (t % 4), ("rs0", t), "gmix"], ["xb%d" % b])
                tv = bankbf(b)
                for c in range(8):
                    S.op("pe", (lambda o, i: (lambda e: e.transpose(o, i, ident[:])))(
                        tv[:, c * 128:(c + 1) * 128], xb[b][:, c * 128:(c + 1) * 128]),
                        ["xb%d" % b, "ident"], [BK(b)], inc=(c == 7))

            def a0_s3(t):
                b = t % 2
                tv = bankbf(b)
                CP("dve", xT[:, :, t * 128:(t + 1) * 128], tv.rearrange("p (c k) -> p c k", k=128),
                   [BK(b)], [("xT", t)])

            a0_s1(0)
            a0_s1(1)
            a0_s1(2)
            a0_s2(0)
            for t in range(NT):
                if t + 3 < NT:
                    a0_s1(t + 3)
                if t + 1 < NT:
                    a0_s2(t + 1)
                a0_s3(t)
            S.barrier()
            S.emit()
            s1.close()
            ropeA = [sbt(sa, "ropeA%d" % i, [128, 384], F32) for i in range(2)]
            ropeB = [sbt(sa, "ropeB%d" % i, [128, 384], F32) for i in range(2)]
            ropeR = [sbt(sa, "ropeR%d" % i, [128, 384], BF16) for i in range(2)]
            pTb = [sbt(sa, "pT%d" % i, [128, 1024], BF16) for i in range(2)]
            den = [sbt(sa, "den%d" % i, [128, 4], F32) for i in range(2)]
            accsb = [sbt(sa, "accsb%d" % i, [128, 4, 260], F32) for i in range(2)]
            odb = [sbt(sa, "od%d" % i, [128, 4, 128], F32) for i in range(2)]
            t1b = sbt(sa, "t1b", [128, 128], F32)
            rrb = [sbt(sa, "rr%d" % i, [128, 4, 2], F32) for i in range(2)]
            msq = [sbt(sa, "msq%d" % i, [128, 4], F32) for i in range(2)]
            rnb = [sbt(sa, "rn%d" % i, [128, 4], F32) for i in range(2)]


            scnt = [0]
            for pi, (kind, idx) in enumerate(passes):
                if kind == "swa":
                    g = idx
                    segs = [(0, 256, g * 256), (256, 64, 512 + g * 64), (320, 64, 512 + g * 64),
                            (384, 64, 640 + g * 64)]
                    ncols, nrope, vd = 448, 384, 64
                    blks = [0, 1, 2]
                else:
                    h = idx
                    segs = [(0, 128, 768 + h * 128), (128, 128, 1280 + h * 128), (256, 128, 1792 + h * 128)]
                    ncols, nrope, vd = 384, 256, 128
                    blks = [0, 2]
                H = nrope // 64
                def proj_mm(t):
                    pb = 2 + t % 2
                    tok = slice(t * 128, (t + 1) * 128)
                    for c in range(8):
                        MM(bank(pb)[:, 0:ncols], xT[:, c, tok], Wsl[:, c, 0:ncols], c == 0, c == 7,
                           [("xT", t), "Wsl"], [BK(pb)], inc=(c == 7))

                def proj_post(t):
                    b = t % 2
                    pb = 2 + b
                    tok = slice(t * 128, (t + 1) * 128)
                    Pv = bank(pb)[:, 0:nrope].rearrange("p (h two i) -> p h two i", two=2, i=32)
                    av = ropeA[b][:, 0:nrope].rearrange("p (h two i) -> p h two i", two=2, i=32)
                    bv = ropeB[b][:, 0:nrope].rearrange("p (h two i) -> p h two i", two=2, i=32)
                    rv = ropeR[b][:, 0:nrope].rearrange("p (h two i) -> p h two i", two=2, i=32)
                    cosb = cosT[:, t, :].unsqueeze(1).unsqueeze(1).to_broadcast([128, H, 2, 32])
                    sinb = sinT[:, t, :].unsqueeze(1).to_broadcast([128, H, 32])
                    TT("dve", av, Pv, cosb, ALU.mult, [BK(pb), "cs"], ["ropeA%d" % b])
                    TT("dve", bv[:, :, 0, :], Pv[:, :, 1, :], sinb, ALU.mult, [BK(pb), "cs"], ["ropeB%d" % b])
                    TT("dve", bv[:, :, 1, :], Pv[:, :, 0, :], sinb, ALU.mult, [BK(pb), "cs"], ["ropeB%d" % b])
                    TT("pool", rv[:, :, 0, :], av[:, :, 0, :], bv[:, :, 0, :], ALU.subtract,
                       ["ropeA%d" % b, "ropeB%d" % b], ["ropeR%d" % b])
                    TT("pool", rv[:, :, 1, :], av[:, :, 1, :], bv[:, :, 1, :], ALU.add,
                       ["ropeA%d" % b, "ropeB%d" % b], ["ropeR%d" % b])
                    ACT(Vp[:, t, 0:vd], bank(pb)[:, nrope:nrope + vd], AF.Copy, [BK(pb)], [("Vp", t)])

                def proj_tr(t):
                    b = t % 2
                    tok = slice(t * 128, (t + 1) * 128)
                    tb = 4 + b
                    tv = bankbf(tb)
                    for i in range(len(blks)):
                        S.op("pe", (lambda o, ii: (lambda e: e.transpose(o, ii, ident[:])))(
                            tv[:, i * 128:(i + 1) * 128], ropeR[b][:, i * 128:(i + 1) * 128]),
                            ["ropeR%d" % b, "ident"], [BK(tb)], inc=(i == len(blks) - 1))
                    if kind == "swa":
                        ACT(QK[:, 0:3, tok], tv[:, 0:384].rearrange("p (c k) -> p c k", k=128), AF.Copy,
                            [BK(tb)], [("QK", 0, t), ("QK", 1, t), ("QK", 2, t)])
                    else:
                        ACT(QK[:, 0, tok], tv[:, 0:128], AF.Copy, [BK(tb)], [("QK", 0, t)])
                        CP("dve", QK[:, 2, tok], tv[:, 128:256], [BK(tb)], [("QK", 2, t)])

                proj_mm(0)
                proj_mm(1)
                proj_post(0)
                for t in range(NT):
                    if t + 2 < NT:
                        proj_mm(t + 2)
                    if t + 1 < NT:
                        proj_post(t + 1)
                    proj_tr(t)
                if pi + 1 < len(passes):
                    load_wsl(pi + 1)
                issue_precast(3)

                if kind == "swa":
                    def swa_qk(n):
                        sb_ = 0 if n % 2 == 0 else 6
                        psS = PS[:, sb_ * 512:(sb_ + 2) * 512].rearrange(
                            "p (par jj kb q) -> p par jj kb q", par=2, jj=2, kb=2)
                        kbs = [1] if n == 0 else [0, 1]
                        qtok = slice(n * 128, (n + 1) * 128)
                        for j in range(4):
                            par, jj = j % 2, j // 2
                            rows = slice(par * 64, par * 64 + 64)
                            for kb in kbs:
                                m = n - 1 + kb
                                MM(psS[:, par, jj, kb, :], QK[rows, 2, m * 128:(m + 1) * 128], QK[rows, jj, qtok],
                                   True, True, [("QK", 2, m), ("QK", jj, n)], [BK(sb_ + par)],
                                   inc=(j == 3 and kb == kbs[-1]))

                    def swa_rest(n):
                        sb_ = 0 if n % 2 == 0 else 6
                        psS = PS[:, sb_ * 512:(sb_ + 2) * 512].rearrange(
                            "p (par jj kb q) -> p par jj kb q", par=2, jj=2, kb=2)
                        kbs = [1] if n == 0 else [0, 1]
                        pk = "pT%d" % (n % 2)
                        pT = pTb[n % 2][:].rearrange("p (par jj kb q) -> p par jj kb q", par=2, jj=2, kb=2)
                        if n == 0:
                            for par in range(2):
                                ACT(pT[:, par, :, 1, :], psS[:, par, :, 1, :], AF.Exp, [BK(sb_ + par)], [pk],
                                    scale=0.125)
                                TT("dve", pT[:, par, :, 1, :], pT[:, par, :, 1, :],
                                   maskSW[:, 1, :].unsqueeze(1).to_broadcast([128, 2, 128]), ALU.mult,
                                   [pk, "maskSW"], [pk])
                        else:
                            ACT(pTb[n % 2][:], PS[:, sb_ * 512:(sb_ + 2) * 512], AF.Exp,
                                [BK(sb_), BK(sb_ + 1)], [pk], scale=0.125)
                            p4 = pTb[n % 2][:].rearrange("p (a kb q) -> p a kb q", a=4, kb=2)
                            TT("dve", p4, p4, maskSW[:].unsqueeze(1).to_broadcast([128, 4, 2, 128]), ALU.mult,
                               [pk, "maskSW"], [pk])
                        ab = 2 + n % 2
                        psA = bank(ab)[:, 0:264].rearrange("p (j d) -> p j d", d=66)
                        for j in range(4):
                            par, jj = j % 2, j // 2
                            for ii, kb in enumerate(kbs):
                                m = n - 1 + kb
                                MM(psA[:, j, 0:65], pT[:, par, jj, kb, :], Vp[:, m, 0:65], ii == 0,
                                   ii == len(kbs) - 1, [pk, ("Vp", m)], [BK(ab)], skip=True,
                                   inc=(j == 3 and ii == len(kbs) - 1))
                    def swa_norm(n):
                        ab = 2 + n % 2
                        psA = bank(ab)[:, 0:264].rearrange("p (j d) -> p j d", d=66)
                        dk = "den%d" % (n % 2)
                        dn = den[n % 2]
                        TT("dve", dn[:], psA[:, :, 64], es[:, 4 * g:4 * g + 4], ALU.add, [BK(ab), "es"], [dk])
                        RCP(dn[:], dn[:], [dk], [dk])
                        TT("dve", O[:, n, g * 256:(g + 1) * 256].rearrange("p (j d) -> p j d", d=64),
                           psA[:, :, 0:64], dn[:].unsqueeze(2).to_broadcast([128, 4, 64]), ALU.mult,
                           [BK(ab), dk], [("O", n)])

                    swa_qk(0)
                    for n in range(NT):
                        if n + 1 < NT:
                            swa_qk(n + 1)
                        swa_rest(n)
                        if n > 0:
                            swa_norm(n - 1)
                    swa_norm(NT - 1)
                else:
                    its = [(J, kb) for J in range(8) for kb in range(4 * J + 4)]
                    bufs = {}

                    def d_qk(i):
                        J, kb = its[i]
                        buf = scnt[0] % 2
                        scnt[0] += 1
                        bufs[i] = buf
                        sbk = 0 if buf == 0 else 2
                        psS = PS[:, sbk * 512:(sbk + 2) * 512].rearrange("p (m q) -> p m q", m=2)
                        v = kb - 4 * J
                        c0 = max(v, 0) * 128
                        for mp in range(2):
                            rows = slice(mp * 64, mp * 64 + 64)
                            MM(psS[:, mp, c0:512], QK[rows, 2, kb * 128:(kb + 1) * 128],
                               QK[rows, 0, J * 512 + c0:(J + 1) * 512], True, True,
                               [("QK", 2, kb)] + [("QK", 0, 4 * J + u) for u in range(c0 // 128, 4)],
                               [BK(sbk + mp)], inc=(mp == 1))

                    def d_rest(i):
                        J, kb = its[i]
                        buf = bufs[i]
                        sbk = 0 if buf == 0 else 2
                        psS = PS[:, sbk * 512:(sbk + 2) * 512].rearrange("p (m q) -> p m q", m=2)
                        v = kb - 4 * J
                        c0 = max(v, 0) * 128
                        pk = "pT%d" % buf
                        pT = pTb[buf][:].rearrange("p (m q) -> p m q", m=2)
                        ACT(pT[:, :, c0:512], psS[:, :, c0:512], AF.Exp, [BK(sbk), BK(sbk + 1)], [pk],
                            scale=0.125)
                        if v >= 0:
                            TT("dve", pT[:, :, c0:c0 + 128], pT[:, :, c0:c0 + 128],
                               maskL[:].unsqueeze(1).to_broadcast([128, 2, 128]), ALU.mult,
                               [pk, "maskL"], [pk])
                        for u in range(max(v, 0), 4):
                            psA = bank(4 + u)[:, 0:260].rearrange("p (m e) -> p m e", m=2)
                            for mp in range(2):
                                MM(psA[:, mp, 0:129], pT[:, mp, u * 128:(u + 1) * 128], Vp[:, kb, 0:129],
                                   kb == 0 and mp == 0, kb == 4 * J + u, [pk, ("Vp", kb)], [BK(4 + u)],
                                   skip=True, inc=(u == 3 and mp == 1))

                    def d_fin_copy(J):
                        jb = J % 2
                        ak = "accsb%d" % jb
                        CP("dve", accsb[jb][:].rearrange("p u (m e) -> p u m e", m=2)[:, :, :, 0:129],
                           PS[:, 2048:4096].rearrange("p (u c) -> p u c", u=4)[:, :, 0:260].rearrange(
                               "p u (m e) -> p u m e", m=2)[:, :, :, 0:129],
                           [BK(4), BK(5), BK(6), BK(7)], [ak])

                    def d_fin(J):
                        jb = J % 2
                        ak = "accsb%d" % jb
                        rk = "rr%d" % jb
                        av4 = accsb[jb][:].rearrange("p u (m e) -> p u m e", m=2)
                        RCP(rrb[jb][:], av4[:, :, :, 128], [ak], [rk])
                        TS("dve", rrb[jb][:, :, 1], rrb[jb][:, :, 1], neglam[:, 0:1], None, ALU.mult, None,
                           [rk, "neglam"], [rk])
                        ok_ = "od%d" % jb
                        for u in range(4):
                            TS("dve", t1b[:], av4[:, u, 0, 0:128], rrb[jb][:, u, 0:1], None, ALU.mult, None,
                               [ak, rk], ["t1b"])
                            STT("dve", odb[jb][:, u, :], av4[:, u, 1, 0:128], rrb[jb][:, u, 1:2], t1b[:],
                                ALU.mult, ALU.add, [ak, rk, "t1b"], [ok_])
                            ACT(junk[:, 0:128], odb[jb][:, u, :], AF.Square, [ok_], [("msq", jb, u)],
                                accum=msq[jb][:, u:u + 1])
                        RSTD(rnb[jb][:], msq[jb][:], 128, [("msq", jb, u) for u in range(4)], ["rn%d" % jb])
                        for u in range(4):
                            STT("dve", O[:, 4 * J + u, 512 + h * 128:512 + (h + 1) * 128], odb[jb][:, u, :],
                                rnb[jb][:, u:u + 1], gd_bc[:], ALU.mult, ALU.mult,
                                [ok_, "rn%d" % jb, "gd"], [("O", 4 * J + u)])

                    d_qk(0)
                    pending = []
                    for i in range(len(its)):
                        if i + 1 < len(its):
                            d_qk(i + 1)
                        d_rest(i)
                        if pending and pending[0][0] <= i:
                            d_fin(pending.pop(0)[1])
                        if its[i][1] == 4 * its[i][0] + 3:
                            d_fin_copy(its[i][0])
                            pending.append((i + 3, its[i][0]))
                    while pending:
                        d_fin(pending.pop(0)[1])
            if DEBUG_O:
                for t in range(NT):
                    ev = DMA("sp", "dbg", dbg[t * 128:(t + 1) * 128, :], O[:, t, :], [("O", t)], [])
            S.barrier()
            S.emit()

        with contextlib.ExitStack() as sbk_:
            Wo = sbt(sbk_, "Wo", [128, 8, DM], BF16)
            Wq = sbt(sbk_, "Wq", [128, 8, DM], BF16)
            Wc = sbt(sbk_, "Wc", [128, 8, DM], BF16)
            wupb = sbt(sbk_, "wupb", [128, 3, 8, 256], BF16)
            wdnb = sbt(sbk_, "wdnb", [128, 3, 2, DM], BF16)
            xr = [sbt(sbk_, "xr%d" % i, [128, DM], F32) for i in range(2)]
            gbcA = sbt(sbk_, "gbcA", [128, DM], F32)
            gbcB = sbt(sbk_, "gbcB", [128, DM], F32)
            gbcC = sbt(sbk_, "gbcC", [128, DM], F32)
            ot = sbt(sbk_, "ot", [128, DM], F32)
            xT12 = sbt(sbk_, "xT12", [128, 8, 256], BF16)
            memKT = sbt(sbk_, "memKT", [128, 8, 256], BF16)
            memV = sbt(sbk_, "memV", [128, 2, DM], BF16)
            qcT = sbt(sbk_, "qcT", [128, 8, 256], BF16)
            pcT = sbt(sbk_, "pcT", [128, 2048], BF16)
            OT2 = [sbt(sbk_, "OT%d" % i, [128, 8, 128], BF16) for i in range(2)]
            ocT2 = [sbt(sbk_, "ocT%d" % i, [128, 8, 128], BF16) for i in range(2)]
            ocn2 = [sbt(sbk_, "ocn%d" % i, [128, DM], BF16) for i in range(2)]
            xbB2 = [sbt(sbk_, "xbB%d" % i, [128, DM], BF16) for i in range(2)]
            xbB = xbB2[0]
            rl = [sbt(sbk_, "rl%d" % i, [128, 256], F32) for i in range(2)]
            hT = [sbt(sbk_, "hT%d" % i, [128, 2, 256], BF16) for i in range(2)]
            rD = sbt(sbk_, "rD", [128, 8], F32)

            def TRB(o, i, R, W, inc=True):
                S.op("pe", lambda e: e.transpose(o, i, ident[:]), list(R) + ["ident"], W, inc=inc)

            def wv(wd):
                return wd.rearrange("(c p) n -> p c n", p=128)

            DMA("pool", "wB", Wo[:], wv(w_out), [], ["Wo"])
            DMA("pool", "wB", Wq[:], wv(w_cq), [], ["Wq"])
            DMA("pool", "wB", Wc[:], wv(w_co), [], ["Wc"])
            DMA("sp", "g0", gbcA[:], g_mem.partition_broadcast(128), [], ["gbcA"])
            DMA("sp", "g1", gbcB[:], g_mlp.partition_broadcast(128), [], ["gbcB"])
            DMA("sp", "g2", gbcC[:], g_final.partition_broadcast(128), [], ["gbcC"])
            MEMSET("dve", ssqs[:], 0.0, [("ssq", i) for i in range(64)])

            wkv = wupb[:, 0:2, :, :].rearrange("p a c n -> p (a c n)").rearrange("p (c n) -> p c n", c=8)
            for mt in range(2):
                DMA("sp", "xB%d" % mt, xr[mt][:], mem[mt * 128:(mt + 1) * 128, :], [], ["xr%d" % mt])
                NORM_CAST(xbB[:], xr[mt][:], gbcA[:], "gbcA", ["xr%d" % mt], ["xbB0"])
                tv = bankbf(mt)
                for c in range(8):
                    TRB(tv[:, c * 128:(c + 1) * 128], xbB[:, c * 128:(c + 1) * 128], ["xbB0"], [BK(mt)], inc=(c == 7))
                ACT(xT12[:, :, mt * 128:(mt + 1) * 128], tv.rearrange("p (c k) -> p c k", k=128), AF.Copy,
                    [BK(mt)], [("xT12", 0), ("xT12", 1)])
            w_ckv_v = wv(w_ckv)
            for q4 in range(4):
                DMA("pool", "wB", wkv, w_ckv_v[:, :, q4 * 512:(q4 + 1) * 512], [], ["wup0", "wup1"])
                for mt in range(2):
                    pb = 2 + mt
                    for c in range(8):
                        MM(bank(pb), xT12[:, c, mt * 128:(mt + 1) * 128], wkv[:, c, :], c == 0, c == 7,
                           [("xT12", 0), ("xT12", 1), "wup0", "wup1"], [BK(pb)], inc=(c == 7))
                    if q4 < 2:
                        CP("dve", xbB[:, 0:512], bank(pb), [BK(pb)], ["xbB0"])
                        tv = bankbf(4 + mt)
                        for c in range(4):
                            TRB(tv[:, c * 128:(c + 1) * 128], xbB[:, c * 128:(c + 1) * 128], ["xbB0"], [BK(4 + mt)], inc=(c == 3))
                        ACT(memKT[:, q4 * 4:(q4 + 1) * 4, mt * 128:(mt + 1) * 128],
                            tv[:, 0:512].rearrange("p (c k) -> p c k", k=128), AF.Copy, [BK(4 + mt)], ["memKT"])
                    else:
                        ACT(memV[:, mt, (q4 - 2) * 512:(q4 - 1) * 512], bank(pb), AF.Copy, [BK(pb)], ["memV"])
            DMA("sp", "g0", gbcA[:], g_cross.partition_broadcast(128), [], ["gbcA"])

            wup_v = wup_s.rearrange("(c p) n -> p c n", p=128)
            NCH = 16
            total_chunks = 16 * NCH

            def load_chunk(k):
                fc = k % NCH
                bb = k % 3
                DMA("sp", "wu%d" % bb, wupb[:, bb, :, :], wup_v[:, :, fc * 256:(fc + 1) * 256],
                    ["wup_s"], ["wup%d" % bb])
                DMA("sp", "wd%d" % bb, wdnb[:, bb, :, :],
                    wdn_s[fc * 256:(fc + 1) * 256, :].rearrange("(f p) n -> p f n", p=128),
                    ["wdn_s"], ["wdn%d" % bb])

            load_chunk(0)
            load_chunk(1)
            kchunk = 0
            for gi in range(16):
                def b1_steps_a(tig):
                    t = 2 * gi + tig
                    X = xr[tig]
                    xk = "xr%d" % tig
                    pbase = 4 * tig
                    OTt, xbt = OT2[tig], xbB2[tig]
                    otk, xbk = "OT%d" % tig, "xbB%d" % tig
                    st_ = []

                    def s0():
                        DMA("sp", "xB%d" % tig, X[:], x[t * 128:(t + 1) * 128, :], [], [xk])
                        tv = bankbf(pbase)
                        for f in range(8):
                            TRB(tv[:, f * 128:(f + 1) * 128], O[:, t, f * 128:(f + 1) * 128], [("O", t)],
                                [BK(pbase)], inc=(f == 7))
                        ACT(OTt[:], tv.rearrange("p (c k) -> p c k", k=128), AF.Copy, [BK(pbase)], [otk])
                    st_.append(s0)

                    def s1():
                        for hf in range(2):
                            pb = pbase + 1 + hf
                            for f in range(8):
                                MM(bank(pb), OTt[:, f, :], Wo[:, f, hf * 512:(hf + 1) * 512], f == 0, f == 7,
                                   [otk, "Wo"], [BK(pb)], inc=(f == 7))
                            TT("dve", X[:, hf * 512:(hf + 1) * 512], bank(pb), X[:, hf * 512:(hf + 1) * 512],
                               ALU.add, [BK(pb), xk], [xk])
                    st_.append(s1)

                    def s2():
                        NORM_CAST(xbt[:], X[:], gbcA[:], "gbcA", [xk], [xbk])
                    st_.append(s2)

                    def s3():
                        tv = bankbf(pbase + 3)
                        for c in range(8):
                            TRB(tv[:, c * 128:(c + 1) * 128], xbt[:, c * 128:(c + 1) * 128], [xbk], [BK(pbase + 3)],
                                inc=(c == 7))
                        ACT(xT12[:, :, tig * 128:(tig + 1) * 128], tv.rearrange("p (c k) -> p c k", k=128),
                            AF.Copy, [BK(pbase + 3)], [("xT12", tig)])
                    st_.append(s3)
                    return st_

                sa0, sa1 = b1_steps_a(0), b1_steps_a(1)
                for i in range(len(sa0)):
                    sa0[i]()
                    sa1[i]()
                for j in range(8):
                    pb = j % 4
                    for c in range(8):
                        MM(bank(pb)[:, 0:256], Wq[:, c, j * 128:(j + 1) * 128], xT12[:, c, :], c == 0, c == 7,
                           ["Wq", ("xT12", 0), ("xT12", 1)], [BK(pb)], inc=(c == 7))
                    CP("act" if j % 2 == 0 else "dve", qcT[:, j, :], bank(pb)[:, 0:256], [BK(pb)], [("qcT", j)])
                psS = PS[:, 2048:4096].rearrange("p (h m q) -> p h m q", h=4, m=2)
                for hc in range(4):
                    for mt in range(2):
                        for dc in range(2):
                            MM(psS[:, hc, mt, :], memKT[:, hc * 2 + dc, mt * 128:(mt + 1) * 128],
                               qcT[:, hc * 2 + dc, :], dc == 0, dc == 1,
                               ["memKT", ("qcT", hc * 2), ("qcT", hc * 2 + 1)], [BK(4 + hc)], skip=True,
                               inc=(mt == 1 and dc == 1))
                    if hc % 2 == 1:
                        h0 = hc - 1
                        ACT(pcT[:, h0 * 512:(h0 + 2) * 512], PS[:, 2048 + h0 * 512:2048 + (h0 + 2) * 512], AF.Exp,
                            [BK(4 + h0), BK(5 + h0)], [("pcT", h0 // 2)], scale=1.0 / 16)
                pc4 = pcT[:].rearrange("p (h m q) -> p h m q", h=4, m=2)

                def b1_steps_b(tig):
                    t = 2 * gi + tig
                    X = xr[tig]
                    xk = "xr%d" % tig
                    ocnt, ocTt, xbt = ocn2[tig], ocT2[tig], xbB2[tig]
                    ock, octk, xbk = "ocn%d" % tig, "ocT%d" % tig, "xbB%d" % tig
                    ob = 2 * tig
                    psO = PS[:, ob * 512:(ob + 2) * 512].rearrange("p (h e) -> p h e", h=4)
                    db = 4 + tig
                    psD = bank(db)
                    trb = 6 + tig
                    st_ = []

                    def s0():
                        for hc in range(4):
                            for mt in range(2):
                                MM(psO[:, hc, :], pc4[:, hc, mt, tig * 128:(tig + 1) * 128],
                                   memV[:, mt, hc * 256:(hc + 1) * 256], mt == 0, mt == 1,
                                   [("pcT", hc // 2), "memV"], [BK(ob + hc // 2)], skip=True, inc=False)
                            for mt in range(2):
                                MM(psD[:, hc:hc + 1], pc4[:, hc, mt, tig * 128:(tig + 1) * 128],
                                   onesc[:, 0:1], mt == 0, mt == 1, [("pcT", hc // 2), "onesc"], [BK(db)],
                                   skip=True, inc=(mt == 1))
                    st_.append(s0)

                    def s1():
                        RCP(rD[:, tig * 4:tig * 4 + 4], psD[:, 0:4], [BK(db)], [("rD", tig)])
                        TT("dve", ocnt[:].rearrange("p (h e) -> p h e", h=4), psO,
                           rD[:, tig * 4:tig * 4 + 4].unsqueeze(2).to_broadcast([128, 4, 256]), ALU.mult,
                           [BK(ob), BK(ob + 1), ("rD", tig)], [ock])
                    st_.append(s1)

                    def s2():
                        tv = bankbf(trb)
                        for c in range(8):
                            TRB(tv[:, c * 128:(c + 1) * 128], ocnt[:, c * 128:(c + 1) * 128], [ock], [BK(trb)],
                                inc=(c == 7))
                        ACT(ocTt[:], tv.rearrange("p (c k) -> p c k", k=128), AF.Copy, [BK(trb)], [octk])
                    st_.append(s2)

                    def s3():
                        for hf in range(2):
                            pb = ob + hf
                            for f in range(8):
                                MM(bank(pb), ocTt[:, f, :], Wc[:, f, hf * 512:(hf + 1) * 512], f == 0, f == 7,
                                   [octk, "Wc"], [BK(pb)], inc=(f == 7))
                            TT("dve", X[:, hf * 512:(hf + 1) * 512], bank(pb), X[:, hf * 512:(hf + 1) * 512],
                               ALU.add, [BK(pb), xk], [xk])
                    st_.append(s3)

                    def s4():
                        NORM_CAST(xbt[:], X[:], gbcB[:], "gbcB", [xk], [xbk])
                    st_.append(s4)

                    def s5():
                        tv = bankbf(trb)
                        for c in range(8):
                            TRB(tv[:, c * 128:(c + 1) * 128], xbt[:, c * 128:(c + 1) * 128], [xbk], [BK(trb)],
                                inc=(c == 7))
                        ACT(xT12[:, :, tig * 128:(tig + 1) * 128], tv.rearrange("p (c k) -> p c k", k=128),
                            AF.Copy, [BK(trb)], [("xT12", tig)])
                    st_.append(s5)
                    return st_

                sb0, sb1 = b1_steps_b(0), b1_steps_b(1)
                for i in range(len(sb0)):
                    sb0[i]()
                    sb1[i]()
                def mlp_up(k):
                    bb = k % 3
                    ubase = 0 if k % 2 == 0 else 2
                    for fs in range(2):
                        ub = ubase + fs
                        for c in range(8):
                            MM(bank(ub)[:, 0:256], wupb[:, bb, c, fs * 128:(fs + 1) * 128], xT12[:, c, :],
                               c == 0, c == 7, ["wup%d" % bb, ("xT12", 0), ("xT12", 1)], [BK(ub)], inc=(c == 7))

                def mlp_act(k):
                    bb = k % 3
                    hb = k % 2
                    ubase = 0 if k % 2 == 0 else 2
                    for fs in range(2):
                        ub = ubase + fs
                        ACT(rl[fs][:], bank(ub)[:, 0:256], AF.Relu, [BK(ub)], ["rl%d" % fs])
                        TT("pool" if fs == 0 else "dve", hT[hb][:, fs, :], rl[fs][:], rl[fs][:], ALU.mult,
                           ["rl%d" % fs], ["hT%d" % hb])

                def mlp_down(k, fc):
                    bb = k % 3
                    hb = k % 2
                    for fs in range(2):
                        for tig in range(2):
                            for hf in range(2):
                                yb = 4 + tig * 2 + hf
                                MM(bank(yb), hT[hb][:, fs, tig * 128:(tig + 1) * 128],
                                   wdnb[:, bb, fs, hf * 512:(hf + 1) * 512],
                                   fc == 0 and fs == 0, fc == NCH - 1 and fs == 1,
                                   ["hT%d" % hb, "wdn%d" % bb], [BK(yb)], inc=(fs == 1 and tig == 1 and hf == 1))

                k0 = kchunk
                mlp_up(k0)
                for fc in range(NCH):
                    k = k0 + fc
                    if k + 2 < total_chunks:
                        load_chunk(k + 2)
                    if fc + 1 < NCH:
                        mlp_up(k + 1)
                    mlp_act(k)
                    mlp_down(k, fc)
                kchunk = k0 + NCH
                for tig in range(2):
                    t = 2 * gi + tig
                    X = xr[tig]
                    xk = "xr%d" % tig
                    for hf in range(2):
                        yb = 4 + tig * 2 + hf
                        TT("dve", X[:, hf * 512:(hf + 1) * 512], bank(yb), X[:, hf * 512:(hf + 1) * 512],
                           ALU.add, [BK(yb), xk], [xk])
                    NORM_CAST_F = norm_ctr[0] % 64
                    norm_ctr[0] += 1
                    i = NORM_CAST_F
                    ACT(junk[:], X[:], AF.Square, [xk], [("ssq", i)], accum=ssqs[:, i:i + 1])
                    RSTD(rss[:, i:i + 1], ssqs[:, i:i + 1], DM, [("ssq", i)], [("rs", i)])
                    STT("dve", ot[:], X[:], rss[:, i:i + 1], gbcC[:], ALU.mult, ALU.mult,
                        [xk, ("rs", i), "gbcC"], ["ot"])
                    DMA("sp", "out", out[t * 128:(t + 1) * 128, :], ot[:], ["ot"], [])
            S.barrier()
            S.emit()
    return nc


_CONST = None


def _consts():
    global _CONST
    if _CONST is None:
        bf = ml_dtypes.bfloat16
        k = np.arange(128)[:, None]
        q = np.arange(128)[None, :]
        ident = (k == q).astype(np.float32).astype(bf)
        maskL = (k <= q).astype(np.float32).astype(bf)
        maskU = (k > q).astype(np.float32).astype(bf)
        maskSW = np.concatenate([maskU, maskL], axis=1)
        invf = (10000.0 ** (-np.arange(0, 64, 2, dtype=np.float64) / 64.0)).astype(np.float32)
        invf = np.ascontiguousarray(np.broadcast_to(invf[None, :], (128, 32)))
        _CONST = dict(ident=ident, maskL=maskL, maskSW=np.ascontiguousarray(maskSW), invf=invf)
    return _CONST


def kernel(x, mem, positions, g_mix, w_in, sinks, lambda_q1, lambda_k1, lambda_q2, lambda_k2,
           g_diff, w_out, g_cross, g_mem, w_cq, w_ckv, w_co, g_mlp, w_up, w_down, g_final):
    f = lambda a: np.ascontiguousarray(np.asarray(a, dtype=np.float32))
    x = f(x); mem = f(mem)
    positions = np.asarray(positions).astype(np.int32)
    shared = dict(_consts())
    shared.update(
        g_mix=f(g_mix)[0], g_cross=f(g_cross)[0], g_mem=f(g_mem)[0], g_mlp=f(g_mlp)[0], g_final=f(g_final),
        g_diff=f(g_diff)[0], sinks=f(sinks)[0], lq1=f(lambda_q1)[0], lk1=f(lambda_k1)[0],
        lq2=f(lambda_q2)[0], lk2=f(lambda_k2)[0],
        w_in=f(w_in)[0], w_out=f(w_out)[0], w_cq=f(w_cq)[0], w_ckv=f(w_ckv)[0], w_co=f(w_co)[0],
        w_up=f(w_up)[0], w_down=f(w_down)[0])
    in_maps = []
    for b in range(8):
        m = dict(shared)
        m["x"] = x[b]
        m["mem"] = mem[b]
        m["pos"] = np.ascontiguousarray(positions[b].reshape(NT, 128).T)
        in_maps.append(m)
    nc = build_nc()
    res = run_bass_kernel_spmd(nc, in_maps, core_ids=list(range(8)))
    outp = np.stack([np.asarray(res.results[b]["out"], dtype=np.float32) for b in range(8)], axis=0)
    if DEBUG_O:
        kernel.dbg = np.stack([np.asarray(res.results[b]["dbg"]).astype(np.float32) for b in range(8)], axis=0)
    return outp
```
